# Optimizing a Trainium2 kernel written in Bass

```python
import math
import jax, jax.numpy as jnp
from jax import lax
import numpy as np

D_MODEL = 2048
BATCH = 4
SEQ = 2048
DEPTH = 1
DEC_BATCH = 32
DEC_SEQ = 16
PAST_LEN = 2048

CHUNK = 64
N_HEADS = 8
D_QK = 64
D_V = 2 * D_QK
ATT_W = N_HEADS * D_V
CONV_W = D_MODEL // 2
CONV_K = 3
NUM_BUCKETS = 32
MAX_DISTANCE = 128
Q_BLOCK = 128
EPS = 1e-6
SPLIT_SIZES = (N_HEADS * 2 * D_QK,
               N_HEADS * 2 * D_QK,
               ATT_W,
               ATT_W,
               CONV_W,
               CONV_W,
               CONV_W,
               CONV_W,
               D_MODEL,
               D_MODEL)
N_IN = sum(SPLIT_SIZES)
SPLIT_IDX = tuple(int(i) for i in np.cumsum(SPLIT_SIZES)[:-1])

kernel_name = "hybrid_diffattn_shortconv_stream_step"


def rmsnorm(x, g):
    xf = x.astype(jnp.float32)
    xf = xf * lax.rsqrt(jnp.mean(xf * xf, axis=-1, keepdims=True) + EPS)
    return xf.astype(x.dtype) * g


def rel_bucket(rel):
    half = NUM_BUCKETS // 2
    max_exact = half // 2
    ret = (rel > 0).astype(jnp.int32) * half
    n = jnp.abs(rel)
    nf = jnp.maximum(n, 1).astype(jnp.float32)
    large = max_exact + (jnp.log(nf / max_exact) / math.log(MAX_DISTANCE / max_exact)
                         * (half - max_exact)).astype(jnp.int32)
    large = jnp.minimum(large, half - 1)
    return ret + jnp.where(n < max_exact, n, large)


def diff_attend(q, k, v, qpos, kpos, lam, rel_bias):
    bias = rel_bias[rel_bucket(kpos[None, :] - qpos[:, None])]
    bias = jnp.transpose(bias, (2, 0, 1)).astype(jnp.float32)
    mask = (kpos[None, :] // CHUNK) <= (qpos[:, None] // CHUNK)
    scale = 1.0 / math.sqrt(D_QK)

    def probs(qa, ka):
        s = jnp.einsum('bqhd,bkhd->bhqk', qa, ka).astype(jnp.float32) * scale + bias
        s = jnp.where(mask, s, jnp.finfo(jnp.float32).min)
        return jax.nn.softmax(s, axis=-1)

    a = probs(q[..., :D_QK], k[..., :D_QK]) - lam * probs(q[..., D_QK:], k[..., D_QK:])
    return jnp.einsum('bhqk,bkhe->bqhe', a.astype(v.dtype), v)


def attend_prompt(q, k, v, lam, rel_bias):
    b, t = q.shape[0], q.shape[1]
    nb = t // Q_BLOCK
    qb = q.reshape(b, nb, Q_BLOCK, N_HEADS, 2 * D_QK).transpose(1, 0, 2, 3, 4)
    starts = jnp.arange(nb, dtype=jnp.int32) * Q_BLOCK
    kpos = jnp.arange(t, dtype=jnp.int32)

    def blk(args):
        qi, s0 = args
        qpos = s0 + jnp.arange(Q_BLOCK, dtype=jnp.int32)
        return diff_attend(qi, k, v, qpos, kpos, lam, rel_bias)

    o = lax.map(blk, (qb, starts))
    return o.transpose(1, 0, 2, 3, 4).reshape(b, t, N_HEADS, D_V)


def short_conv(u, state, w):
    t = u.shape[1]
    p = jnp.concatenate([state, u], axis=1)
    y = w[0] * p[:, 0:t] + w[1] * p[:, 1:t + 1] + w[2] * p[:, 2:t + 2]
    return y, p[:, -(CONV_K - 1):]


def mixer_layer(x, past_k, past_v, conv_state, is_prompt, lam, lam_init,
                norm_pre, norm_post, w_in, head_norm, conv_w,
                w_proj_attn, w_proj_conv, w_out, rel_bias):
    b, t, _ = x.shape
    xn = rmsnorm(x, norm_pre)
    p = jnp.einsum('btd,dn->btn', xn, w_in)
    q, k, v, za, bg, cg, h, zc, ga, gc = jnp.split(p, SPLIT_IDX, axis=-1)
    q = q.reshape(b, t, N_HEADS, 2 * D_QK)
    k = k.reshape(b, t, N_HEADS, 2 * D_QK)
    v = v.reshape(b, t, N_HEADS, D_V)
    if is_prompt:
        o = attend_prompt(q, k, v, lam, rel_bias)
    else:
        k_all = jnp.concatenate([past_k, k], axis=1)
        v_all = jnp.concatenate([past_v, v], axis=1)
        past = past_k.shape[1]
        qpos = past + jnp.arange(t, dtype=jnp.int32)
        kpos = jnp.arange(past + t, dtype=jnp.int32)
        o = diff_attend(q, k_all, v_all, qpos, kpos, lam, rel_bias)
    o = rmsnorm(o, head_norm) * (1.0 - lam_init)
    ya = jnp.einsum('bte,ed->btd', o.reshape(b, t, ATT_W) * jax.nn.silu(za), w_proj_attn)
    yc, new_conv = short_conv(cg * h, conv_state, conv_w)
    yc = jnp.einsum('btc,cd->btd', bg * yc * jax.nn.silu(zc), w_proj_conv)
    y = jnp.einsum('btd,de->bte', jax.nn.sigmoid(ga) * ya + jax.nn.sigmoid(gc) * yc, w_out)
    return x + rmsnorm(y, norm_post), k, v, new_conv


def setup_inputs(seed: int = 0) -> dict:
    key = jax.random.key(seed)
    ks = jax.random.split(key, 20)
    f32 = jnp.float32
    nrm = lambda k, s, sc: jax.random.normal(k, s, f32) * sc
    return {
        "x_prompt": nrm(ks[0], (BATCH, SEQ, D_MODEL), 1.0),
        "x_sample": nrm(ks[1], (DEC_BATCH, DEC_SEQ, D_MODEL), 1.0),
        "cache_k": nrm(ks[2], (DEPTH, DEC_BATCH, PAST_LEN, N_HEADS, 2 * D_QK), 1.0),
        "cache_v": nrm(ks[3], (DEPTH, DEC_BATCH, PAST_LEN, N_HEADS, D_V), 1.0),
        "state_conv": nrm(ks[4], (DEPTH, DEC_BATCH, CONV_K - 1, CONV_W), 1.0),
        "norm_pre": 1.0 + nrm(ks[5], (DEPTH, D_MODEL), 0.02),
        "norm_post": 1.0 + nrm(ks[6], (DEPTH, D_MODEL), 0.02),
        "w_in": nrm(ks[7], (DEPTH, D_MODEL, N_IN), D_MODEL ** -0.5),
        "lambda_q1": nrm(ks[8], (DEPTH, D_QK), 0.1),
        "lambda_k1": nrm(ks[9], (DEPTH, D_QK), 0.1),
        "lambda_q2": nrm(ks[10], (DEPTH, D_QK), 0.1),
        "lambda_k2": nrm(ks[11], (DEPTH, D_QK), 0.1),
        "head_norm": 1.0 + nrm(ks[12], (DEPTH, D_V), 0.02),
        "conv_w": nrm(ks[13], (DEPTH, CONV_K, CONV_W), CONV_K ** -0.5),
        "w_proj_attn": nrm(ks[14], (DEPTH, ATT_W, D_MODEL), ATT_W ** -0.5),
        "w_proj_conv": nrm(ks[15], (DEPTH, CONV_W, D_MODEL), CONV_W ** -0.5),
        "w_out": nrm(ks[16], (DEPTH, D_MODEL, D_MODEL), D_MODEL ** -0.5),
        "rel_bias": nrm(ks[17], (NUM_BUCKETS, N_HEADS), 0.5),
    }


def reference(x_prompt, x_sample, cache_k, cache_v, state_conv, norm_pre, norm_post,
              w_in, lambda_q1, lambda_k1, lambda_q2, lambda_k2, head_norm, conv_w,
              w_proj_attn, w_proj_conv, w_out, rel_bias):
    xp, xs = x_prompt, x_sample
    kp_l, vp_l, cp_l, ks_l, vs_l, cs_l = [], [], [], [], [], []
    zero_conv = jnp.zeros((xp.shape[0], CONV_K - 1, CONV_W), xp.dtype)
    for l in range(DEPTH):
        lam_init = 0.8 - 0.6 * math.exp(-0.3 * l)
        lam = (jnp.exp(jnp.sum(lambda_q1[l].astype(jnp.float32) * lambda_k1[l].astype(jnp.float32)))
               - jnp.exp(jnp.sum(lambda_q2[l].astype(jnp.float32) * lambda_k2[l].astype(jnp.float32)))
               + lam_init)
        weights = (norm_pre[l], norm_post[l], w_in[l], head_norm[l], conv_w[l],
                   w_proj_attn[l], w_proj_conv[l], w_out[l], rel_bias)
        xp, kp, vp, cp = mixer_layer(xp, None, None, zero_conv, True, lam, lam_init, *weights)
        xs, ksm, vsm, csm = mixer_layer(xs, cache_k[l], cache_v[l], state_conv[l], False,
                                        lam, lam_init, *weights)
        kp_l.append(kp); vp_l.append(vp); cp_l.append(cp)
        ks_l.append(ksm); vs_l.append(vsm); cs_l.append(csm)
    new_k_prompt = jnp.stack(kp_l, axis=0)
    new_v_prompt = jnp.stack(vp_l, axis=0)
    new_conv_prompt = jnp.stack(cp_l, axis=0)
    new_k_sample = jnp.stack(ks_l, axis=0)
    new_v_sample = jnp.stack(vs_l, axis=0)
    new_conv_sample = jnp.stack(cs_l, axis=0)
    return (xp, xs, new_k_prompt, new_v_prompt, new_conv_prompt,
            new_k_sample, new_v_sample, new_conv_sample)
```

```python
import math
import numpy as np
import concourse.bass as bass
import concourse.mybir as mybir
from concourse.bass_utils import run_bass_kernel_spmd

F32 = mybir.dt.float32
BF16 = mybir.dt.bfloat16
AF = mybir.ActivationFunctionType
ALU = mybir.AluOpType

D = 2048
NH = 8
NCORES = 8
EPS = 1e-6
NOWN = 1024
NCOL = 1104
CH0 = 1024
CS0 = 1040
SBASE = 16512

LAST_PHASE = "F"
DEBUG_DUMPS = False

D_SLOTS = 8
D_HEADS = 8
D_SAMPLE = True
D_EPI = True
PH_ORDER = ["S", "A", "B", "C1", "D", "C2", "E", "F"]


def _ph(name):
    return PH_ORDER.index(name) <= PH_ORDER.index(LAST_PHASE)


class Sched:
    ENG = ("pe", "act", "dve", "pool", "sp")

    def __init__(self):
        self.q = {e: [] for e in self.ENG}
        self.cnt = {e: 0 for e in self.ENG}
        self.dcnt = {}
        self.waited = {e: {} for e in self.ENG}
        self.last = {e: None for e in self.ENG}

    def wait(self, eng, ms):
        if ms is None:
            return
        if isinstance(ms, list):
            for m in ms:
                self.wait(eng, m)
            return
        key, val = ms
        assert key != "never", "waiting on an unresolved deferred milestone"
        if self.waited[eng].get(key, 0) >= val:
            return
        self.waited[eng][key] = val
        self.q[eng].append(("wait", key, val))

    def op(self, eng, fn, deps=(), inc=True):
        for d in deps:
            self.wait(eng, d)
        ms = None
        if inc:
            self.cnt[eng] += 1
            ms = (eng, self.cnt[eng])
            self.last[eng] = ms
        self.q[eng].append(("op", fn, inc))
        return ms

    def dma(self, eng, semkey, out, in_, deps=(), **kw):
        for d in deps:
            self.wait(eng, d)
        self.dcnt[semkey] = self.dcnt.get(semkey, 0) + 16
        self.q[eng].append(("dma", out, in_, semkey, kw))
        return (semkey, self.dcnt[semkey])

    def all_last(self):
        return [m for m in self.last.values() if m is not None]


def _np_bucket(rel):
    half = 16
    max_exact = 8
    ret = (rel > 0).astype(np.int32) * half
    n = np.abs(rel)
    nf = np.maximum(n, 1).astype(np.float32)
    large = max_exact + (np.log(nf / np.float32(max_exact)) / np.float32(math.log(128 / max_exact))
                         * np.float32(half - max_exact)).astype(np.int32)
    large = np.minimum(large, half - 1)
    return ret + np.where(n < max_exact, n, large)


def _onehot_window(offset):
    rel = np.arange(255, dtype=np.int32) - 127 + offset
    b = _np_bucket(rel)
    oh = np.zeros((32, 255), np.float32)
    oh[b, np.arange(255)] = 1.0
    return oh


def _w_in_perm():
    cols = []
    cols += list(range(1024, 2048))
    cols += list(range(2048, 3072))
    cols += list(range(0, 1024))
    cols += list(range(3072, 4096))
    for c in range(8):
        for base in (4096, 5120, 6144, 7168):
            cols += list(range(base + 128 * c, base + 128 * c + 128))
    for g in range(8):
        cols += list(range(8192 + 256 * g, 8192 + 256 * g + 256))
        cols += list(range(10240 + 256 * g, 10240 + 256 * g + 256))
    return np.asarray(cols, np.int64)


def build_program():
    nc = bass.Bass("TRN2", target_bir_lowering=False)
    S = Sched()

    def din(name, shape, dt=F32):
        return nc.dram_tensor(name, list(shape), dt, kind="ExternalInput").ap()

    def dout(name, shape, dt=F32):
        return nc.dram_tensor(name, list(shape), dt, kind="ExternalOutput").ap()

    xo = din("xo", [NOWN, D])
    xt = din("xt", [NOWN, D])
    xhs = din("xhs", [80, D])
    ck = din("ck", [4, 4, 128, 16, 256])
    cv = din("cv", [4, 4, 128, 16, 256])
    sc = din("sc", [4, 2, 1024])
    w_l = din("w_l", [24, 128, 16, 512])
    wpa_l = din("wpa_l", [8, 128, 8, 256])
    wpc_l = din("wpc_l", [8, 128, 8, 256])
    wo_l = din("wo_l", [128, 16, 2048])
    g_pre = din("g_pre", [1, D])
    g_post = din("g_post", [1, D])
    g_head = din("g_head", [1, 128])
    conv_w = din("conv_w", [3, 1024])
    lam_in = din("lam_in", [1, 256])
    rel_b = din("rel_b", [32, 8])
    c_ident = din("c_ident", [128, 128])
    c_ohd = din("c_ohd", [64, 255])
    c_ohp = din("c_ohp", [64, 255])
    c_maskd = din("c_maskd", [128, 128])
    c_bmask = din("c_bmask", [64, 64])
    c_par = din("c_par", [128, 1])

    yo = dout("yo", [NOWN, D])
    ys = dout("ys", [64, D])
    ko = dout("ko", [NOWN, 1024])
    vo = dout("vo", [NOWN, 1024])
    kso = dout("kso", [64, 1024])
    vso = dout("vso", [64, 1024])
    cvo = dout("cvo", [5, 2, 1024])
    dbg = {}

    def sb(name, shape, dt, off):
        nbytes = int(np.prod(shape[1:])) * (4 if dt == F32 else 2)
        assert off % 32 == 0, (name, off)
        assert SBASE + off + nbytes <= 229376, (name, off, nbytes)
        return nc.alloc_sbuf_tensor_at(name, list(shape), dt, offset=SBASE + off), off + ((nbytes + 31) // 32) * 32

    o = 0
    identf, o = sb("identf", [128, 128], F32, o)
    identb, o = sb("identb", [128, 128], BF16, o)
    bias_t = {}
    for nm_ in ("D", "P", "X1", "X2"):
        hi_, o = sb(f"b{nm_}h", [128, NH, 128], BF16, o)
        lo_, o = sb(f"b{nm_}l", [128, NH, 128], BF16, o)
        bias_t[nm_] = (hi_, lo_)
    bnew_h, o = sb("bnewh", [64, NH, 64], BF16, o)
    bnew_l, o = sb("bnewl", [64, NH, 64], BF16, o)
    negpar, o = sb("negpar", [128, 1], F32, o)
    ompar, o = sb("ompar", [128, 1], F32, o)
    hnb, o = sb("hnb", [128, 128], F32, o)
    convw, o = sb("convw", [128, 8, 3], F32, o)
    maskd, o = sb("maskd", [128, 128], F32, o)
    bmask, o = sb("bmask", [64, 64], F32, o)
    par, o = sb("par", [128, 1], F32, o)
    rb, o = sb("rb", [64, 8], F32, o)
    rb15, o = sb("rb15", [64, 8], F32, o)
    rbs, o = sb("rbs", [64, 8], F32, o)
    rbhf, o = sb("rbhf", [64, 8], F32, o)
    rbh, o = sb("rbh", [64, 8], BF16, o)
    rbs2, o = sb("rbs2", [64, 8], BF16, o)
    ohd, o = sb("ohd", [64, 255], F32, o)
    ohp, o = sb("ohp", [64, 255], F32, o)
    ohdb, o = sb("ohdb", [64, 256], BF16, o)
    ohpb, o = sb("ohpb", [64, 256], BF16, o)
    scT, o = sb("scT", [128, 8, 4, 2], F32, o)
    ncv, o = sb("ncv", [128, 8, 5, 2], F32, o)
    lamv, o = sb("lamv", [128, 256], F32, o)
    lams, o = sb("lams", [128, 8], F32, o)
    ssA, o = sb("ssA", [128, 20], F32, o)
    rsA, o = sb("rsA", [128, 20], F32, o)
    ssF, o = sb("ssF", [128, 9, 4], F32, o)
    ssF1, o = sb("ssF1", [128, 9], F32, o)
    rsF, o = sb("rsF", [128, 9], F32, o)
    epsD, o = sb("epsD", [128, 1], F32, o)
    eps128, o = sb("eps128", [128, 1], F32, o)
    est, o = sb("est", [128, 8, 8], F32, o)
    assert o <= 26624, o
    OFF_XNT = 26624
    xnT, _ = sb("xnT", [128, 16, NCOL], BF16, OFF_XNT)
    OFF_XO = 61952
    xnT_oth, _ = sb("xnT_oth", [128, 16, 1024], BF16, OFF_XO)
    QT, _ = sb("QT", [128, NH, NCOL], BF16, OFF_XO)
    zas, _ = sb("zas", [128, 8, 1024], BF16, OFF_XO + 17664)
    zass, _ = sb("zass", [16, 4, 1024], BF16, OFF_XO + 17664 + 16384)
    OFF_S2 = 104192
    NXST = 3
    OFF_KT_ = OFF_S2 + 32768
    OFF_VA_ = OFF_KT_ + 34048
    xst = [sb(f"xst{i}", [128, D], F32, OFF_VA_ + 8192 * i)[0] for i in range(NXST)]
    xnb = [sb(f"xnb{i}", [128, D], BF16, OFF_VA_ + 24576 + 4096 * i)[0] for i in range(2)]
    gpre, _ = sb("gpre", [128, D], F32, OFF_KT_)
    bf_D, _ = sb("bf_D", [128, NH, 128], F32, OFF_KT_ + 8192)
    bf_P, _ = sb("bf_P", [128, NH, 128], F32, OFF_KT_ + 12288)
    bf_T, _ = sb("bf_T", [128, NH, 128], F32, OFF_KT_ + 16384)
    bf_U, _ = sb("bf_U", [128, NH, 128], F32, OFF_KT_ + 20480)
    negm, _ = sb("negm", [128, 128], F32, OFF_KT_ + 24576)
    negb, _ = sb("negb", [64, 64], F32, OFF_KT_ + 25088)
    wsl = [sb(f"wsl{i}", [128, 16, 512], BF16, OFF_S2 + 16384 * i)[0] for i in range(2)]
    OFF_KT = OFF_S2 + 32768
    KT, _ = sb("KT", [128, NH, 2128], BF16, OFF_KT)
    OFF_VA = OFF_KT + 34048
    Vaug, _ = sb("Vaug", [128, 16, NH, 129], BF16, OFF_VA)
    OFF_VN = OFF_VA + 33024
    vnew, _ = sb("vnew", [64, NH, 129], BF16, OFF_VN)
    OFF_KVO = OFF_VN + 2080
    kvst = [sb(f"kvst{i}", [128, 512], F32, OFF_KVO + 2048 * i)[0] for i in range(3)]
    assert OFF_KVO + 6144 <= 212864

    o = OFF_S2
    Ksh = []
    Vsh = []
    KTsh = []
    for i in range(1):
        t, o = sb(f"KTsh{i}", [128, 2048], BF16, o)
        KTsh.append(t)
    Pt = []
    for i in range(3):
        t, o = sb(f"Pt{i}", [128, 2, 4, 128], BF16, o)
        Pt.append(t)
    Ptn_all, o = sb("Ptn_all", [64, NH, 2, 64], BF16, o)
    K0s, o = sb("K0s", [128, 16, 256], BF16, o)
    V0s, o = sb("V0s", [128, 16, 2, 129], BF16, o)
    assert o <= OFF_S2 + 32768, o
    o = OFF_KVO
    ep_t = []
    ep_d = []
    ep_oz = []
    for i in range(2):
        t, o = sb(f"ep_t{i}", [128, 128], F32, o)
        ep_t.append(t)
    for i in range(2):
        t, o = sb(f"ep_d{i}", [128, 128], F32, o)
        ep_d.append(t)
    for i in range(2):
        t, o = sb(f"ep_oz{i}", [128, 128], BF16, o)
        ep_oz.append(t)
    ep_junk, o = sb("ep_junk", [128, 128], F32, o)
    Pts = []
    for i in range(2):
        t, o = sb(f"Pts{i}", [128, 2, 16, 16], BF16, o)
        Pts.append(t)
    Ptn, o = sb("Ptn", [64, 2, 64], BF16, o)
    assert o <= 212864, o

    Vf2 = [sb(f"Vf2_{i}", [128, 16, 256], F32, OFF_KT + 16384 * i)[0] for i in range(2)]
    Ksh2 = [sb(f"Ksh2_{i}", [128, 16, 256], BF16, OFF_VA + 8192 * i)[0] for i in range(2)]
    Vsh2 = [sb(f"Vsh2_{i}", [128, 16, 2, 129], BF16, OFF_VA + 16384 + 8256 * i)[0] for i in range(2)]
    assert 16384 + 2 * 8256 <= 33024
    rT, _ = sb("rT", [128, 8, NCOL], BF16, OFF_KT)
    o = OFF_VA
    ubuf = []
    gbuf = []
    for i in range(2):
        t, o = sb(f"ubuf{i}", [128, 1112], F32, o)
        ubuf.append(t)
    for i in range(2):
        t, o = sb(f"gbuf{i}", [128, NCOL], F32, o)
        gbuf.append(t)
    c2h = []
    c2z = []
    for i in range(2):
        t, o = sb(f"c2h{i}", [128, 512], F32, o)
        c2h.append(t)
    for i in range(2):
        t, o = sb(f"c2z{i}", [128, 512], F32, o)
        c2z.append(t)
    ybuf, o = sb("ybufc", [128, NCOL], F32, o)
    assert o <= OFF_KVO, o

    wpa = [sb(f"wpa{i}", [128, 8, 256], BF16, OFF_KT + 17664 + 4096 * i)[0] for i in range(2)]
    wpc = [sb(f"wpc{i}", [128, 8, 256], BF16, OFF_KT + 17664 + 8192 + 4096 * i)[0] for i in range(2)]
    assert OFF_KT + 17664 + 16384 <= OFF_VA
    mT, _ = sb("mT", [128, 16, NCOL], BF16, OFF_VA)
    assert OFF_VA + 35328 <= 212864
    o = OFF_XO + 17664
    e_sg = []
    e_t = []
    for i in range(4):
        t, o = sb(f"e_sg{i}", [128, 512], F32, o)
        e_sg.append(t)
    for i in range(4):
        t, o = sb(f"e_t{i}", [128, 512], F32, o)
        e_t.append(t)
    assert o <= OFF_S2

    y_acc, _ = sb("y_acc", [128, 9, D], F32, OFF_XNT)
    assert OFF_XNT + 73728 <= OFF_S2
    gpost, _ = sb("gpost", [128, D], F32, OFF_KT)
    xh = [sb(f"xh{i}", [128, 1024], F32, OFF_KT + 8192 + 4096 * i)[0] for i in range(2)]
    yh = [sb(f"yh{i}", [128, 1024], F32, OFF_KT + 16384 + 4096 * i)[0] for i in range(2)]
    fjunk, _ = sb("fjunk", [128, 512], F32, OFF_KT + 24576)

    banks = [nc.alloc_psum_tensor(f"bank{i}", [128, 512], F32) for i in range(8)]

    def bk(i):
        return banks[i][:, :]

    def bk16(i):
        return banks[i][:, :].bitcast(BF16)

    bank_free = [None] * 8

    ld = {}
    ld["ident"] = S.dma("sp", "c0", identf[:, :], c_ident)
    S.dma("sp", "c1", rb[0:32, :], rel_b)
    ld["rb"] = S.dma("sp", "c1", rb[32:64, :], rel_b)
    ld["rb15"] = S.dma("sp", "c2", rb15[:, :], rel_b[15:16, :].partition_broadcast(64))
    ld["ohd"] = S.dma("sp", "c3", ohd[:, :], c_ohd)
    ld["ohp"] = S.dma("sp", "c4", ohp[:, :], c_ohp)
    ld["maskd"] = S.dma("sp", "c5", maskd[:, :], c_maskd)
    ld["bmask"] = S.dma("sp", "c6", bmask[:, :], c_bmask)
    ld["par"] = S.dma("sp", "c7", par[:, :], c_par)
    ld["gpre"] = S.dma("sp", "c8", gpre[:, :], g_pre.partition_broadcast(128))
    ld["hn"] = S.dma("sp", "c9", hnb[:, :], g_head.partition_broadcast(128))
    ld["lam"] = S.dma("sp", "c10", lamv[:, :], lam_in.partition_broadcast(128))

    m_identb = S.op("dve", lambda e: e.tensor_copy(identb[:, :], identf[:, :]), deps=[ld["ident"]])
    m_epsD = S.op("dve", lambda e: e.memset(epsD[:, :], float(D * EPS)))
    m_eps128 = S.op("dve", lambda e: e.memset(eps128[:, :], float(128.0 * EPS)))
    m_rbs0 = S.op("dve", lambda e: e.tensor_tensor(rbs[:, :], rb[:, :], rb15[:, :], ALU.subtract),
                  deps=[ld["rb"], ld["rb15"]])
    m_rbs0 = S.op("dve", lambda e: e.tensor_scalar(rbs[:, :], rbs[:, :], 8.0, None, ALU.mult), deps=[m_rbs0])
    m_r1 = S.op("dve", lambda e: e.tensor_copy(rbh[:, :], rbs[:, :]), deps=[m_rbs0])
    m_r2 = S.op("dve", lambda e: e.tensor_copy(rbhf[:, :], rbh[:, :]), deps=[m_r1])
    m_r3 = S.op("dve", lambda e: e.tensor_tensor(rbhf[:, :], rbs[:, :], rbhf[:, :], ALU.subtract), deps=[m_r2])
    m_r4 = S.op("dve", lambda e: e.tensor_copy(rbs2[0:32, :], rbh[0:32, :]), deps=[m_r1])
    m_r5 = S.op("dve", lambda e: e.tensor_copy(rbs2[32:64, :], rbhf[32:64, :]), deps=[m_r3])
    m_o1 = S.op("dve", lambda e: e.tensor_copy(ohdb[:, 0:255], ohd[:, :]), deps=[ld["ohd"]])
    m_o2 = S.op("dve", lambda e: e.tensor_copy(ohpb[:, 0:255], ohp[:, :]), deps=[ld["ohp"]])
    m_rbs = [m_r4, m_r5, m_o1, m_o2]
    m_gpre = S.op("dve", lambda e: e.tensor_scalar(gpre[:, :], gpre[:, :], float(math.sqrt(D)), None, ALU.mult),
                  deps=[ld["gpre"]])
    m_hnb = S.op("dve", lambda e: e.tensor_scalar(hnb[:, :], hnb[:, :], float(0.8 * math.sqrt(128.0)), None, ALU.mult),
                 deps=[ld["hn"]])
    m = S.op("dve", lambda e: e.scalar_tensor_tensor(lamv[:, 0:64], lamv[:, 0:64], 1.0, lamv[:, 64:128],
                                                      ALU.mult, ALU.mult, accum_out=lams[:, 0:1]),
             deps=[ld["lam"]])
    m2 = S.op("dve", lambda e: e.scalar_tensor_tensor(lamv[:, 128:192], lamv[:, 128:192], 1.0, lamv[:, 192:256],
                                                       ALU.mult, ALU.mult, accum_out=lams[:, 1:2]),
              deps=[ld["lam"]])
    m3 = S.op("act", lambda e: e.activation(lams[:, 2:4], lams[:, 0:2], AF.Exp), deps=[m, m2])
    m4 = S.op("dve", lambda e: e.tensor_tensor(lams[:, 4:5], lams[:, 2:3], lams[:, 3:4], ALU.subtract), deps=[m3])
    m_lam = S.op("dve", lambda e: e.tensor_scalar(lams[:, 5:6], lams[:, 4:5], 0.2, -1.0, ALU.add, ALU.mult), deps=[m4])

    m_ones2 = S.op("pool", lambda e: e.memset(vnew[:, :, 128:129], 1.0))

    a_done = []
    if _ph("A"):
        tiles = [("own", i) for i in range(8)] + [("hs", 0)] + [("oth", i) for i in range(8)]
        xst_free = [None] * NXST
        xnb_free = [None, None]
        tr_bank = [4, 5]
        tinfo = {}

        def stage_a1(ti):
            kind, i = tiles[ti]
            b = ti % 2
            bx = ti % NXST
            rows = 80 if kind == "hs" else 128
            src = {"own": xo, "oth": xt, "hs": xhs}[kind]
            srows = src[i * 128:(i + 1) * 128, :] if kind != "hs" else src[:, :]
            m_ld = S.dma("sp", f"xst{bx}", xst[bx][0:rows, :], srows, deps=[xst_free[bx]])
            xs_ = xst[bx]
            xb_ = xnb[b]
            m_sq = S.op("act", (lambda e: e.activation(
                xb_[0:rows, :], xs_[0:rows, :], AF.Square, accum_out=ssA[0:rows, ti:ti + 1])),
                deps=[m_ld, xnb_free[b]])
            m_ln = S.op("act", (lambda e: e.activation(
                rsA[0:rows, ti:ti + 1], ssA[0:rows, ti:ti + 1], AF.Ln, bias=epsD[0:rows, 0:1])),
                deps=[m_sq, m_epsD])
            m_rs = S.op("act", (lambda e: e.activation(
                rsA[0:rows, ti:ti + 1], rsA[0:rows, ti:ti + 1], AF.Exp, scale=-0.5)),
                deps=[m_ln])
            m_xn = S.op("dve", (lambda e: e.scalar_tensor_tensor(
                xb_[0:rows, :], xs_[0:rows, :], rsA[0:rows, ti:ti + 1], gpre[0:rows, :], ALU.mult, ALU.mult)),
                deps=[m_rs, m_gpre])
            xst_free[bx] = m_xn
            tinfo[ti] = (m_xn, xb_, rows, b)

        def stage_a2(ti):
            kind, i = tiles[ti]
            m_xn, xb_, rows, b = tinfo[ti]
            lastT = None
            for half in range(2):
                bnk = tr_bank[half]
                pv = bk16(bnk).rearrange("p (c t) -> p c t", t=128)
                mm = None
                for cc in range(8):
                    dc = half * 8 + cc
                    outp = pv[:, cc, 0:rows]
                    inp = xb_[0:rows, dc * 128:(dc + 1) * 128]
                    idn = identb[0:rows, 0:rows]
                    mm = S.op("pe", (lambda e, outp=outp, inp=inp, idn=idn: e.transpose(outp, inp, idn)),
                              deps=[m_xn, m_identb, bank_free[bnk]] if cc == 0 else [], inc=(cc == 7))
                lastT = mm
                if kind == "oth":
                    dst = xnT_oth[:, half * 8:(half + 1) * 8, i * 128:(i + 1) * 128]
                elif kind == "own":
                    dst = xnT[:, half * 8:(half + 1) * 8, i * 128:(i + 1) * 128]
                else:
                    dst = xnT[:, half * 8:(half + 1) * 8, CH0:CH0 + 80]
                srcp = pv[:, :, 0:rows]
                if half == 0:
                    mev = S.op("act", (lambda e, dst=dst, srcp=srcp: e.copy(dst, srcp)), deps=[mm])
                else:
                    mev = S.op("dve", (lambda e, dst=dst, srcp=srcp: e.tensor_copy(dst, srcp)), deps=[mm])
                bank_free[bnk] = mev
                a_done.append(mev)
            xnb_free[b] = lastT

        stage_a1(0)
        for ti in range(len(tiles)):
            if ti + 1 < len(tiles):
                stage_a1(ti + 1)
            stage_a2(ti)
        a_done = a_done[-6:] + [S.last["act"], S.last["dve"]]
    for k_ in range(3):
        ld["convw"] = S.dma("sp", "c11", convw[:, :, k_], conv_w[k_].rearrange("(c p) -> p c", p=128),
                            allow_slow_non_contiguous=True)
    for s_ in range(4):
        for j_ in range(2):
            ld["scT"] = S.dma("sp", "c12", scT[:, :, s_, j_], sc[s_, j_].rearrange("(c p) -> p c", p=128),
                              allow_slow_non_contiguous=True)
    m_ones = S.op("pool", lambda e: e.memset(Vaug[:, :, :, 128:129], 1.0), deps=a_done)
    BIG = 30000.0

    def gen_bias(oh, dstf, bank_a, bank_b):
        pa = bk(bank_a).rearrange("p (q h) -> p q h", h=8)
        pb = bk(bank_b).rearrange("p (q h) -> p q h", h=8)
        last = None
        for q in range(128):
            pbk = pa if q < 64 else pb
            qq = q % 64
            lhsT = oh[:, 127 - q:255 - q]
            outp = pbk[:, qq, :]
            islast = (q == 63 or q == 127)
            mm = S.op("pe", (lambda e, outp=outp, lhsT=lhsT: e.matmul(outp, lhsT, rbs2[:, :], start=True, stop=True)),
                      deps=[m_rbs, bank_free[bank_a], bank_free[bank_b]] + a_done if q == 0 else [],
                      inc=islast)
            if islast:
                last = mm
        ms = []
        for half, pbk in ((0, pa), (1, pb)):
            src = pbk.rearrange("p q h -> p h q")
            d = dstf[:, :, half * 64:(half + 1) * 64]
            ms.append(S.op("act", (lambda e, d=d, src=src: e.copy(d, src)), deps=[last] + a_done))
        bank_free[bank_a] = ms[0]
        bank_free[bank_b] = ms[1]
        return ms

    mD = gen_bias(ohdb, bf_D, 0, 1)
    mP = gen_bias(ohpb, bf_P, 2, 3)
    m_np = S.op("dve", lambda e: e.tensor_scalar(negpar[:, :], par[:, :], -1.0, BIG, ALU.add, ALU.mult), deps=[ld["par"]])
    m_op = S.op("dve", lambda e: e.tensor_scalar(ompar[:, :], par[:, :], -1.0, 1.0, ALU.mult, ALU.add), deps=[ld["par"]])
    m_ng = S.op("dve", lambda e: e.tensor_scalar(negm[:, :], maskd[:, :], -1.0, BIG, ALU.add, ALU.mult), deps=[ld["maskd"]] + a_done)
    m_nb = S.op("dve", lambda e: e.tensor_scalar(negb[:, :], bmask[:, :], -1.0, BIG, ALU.add, ALU.mult), deps=[ld["bmask"]] + a_done)

    def split_hilo(srcf, hi_, lo_, tmpf_, deps):
        h1 = S.op("dve", lambda e: e.tensor_copy(hi_, srcf), deps=deps)
        h2 = S.op("dve", lambda e: e.tensor_copy(tmpf_, hi_), deps=[h1])
        h3 = S.op("dve", lambda e: e.tensor_tensor(tmpf_, srcf, tmpf_, ALU.subtract), deps=[h2])
        h4 = S.op("dve", lambda e: e.tensor_copy(lo_, tmpf_), deps=[h3])
        return [h1, h4]

    n1 = S.op("dve", lambda e: e.tensor_tensor(bf_T[0:64, :, 0:64], bf_D[0:64, :, 0:64],
                                               bmask[:, :].unsqueeze(1).to_broadcast([64, NH, 64]), ALU.mult), deps=mD + [ld["bmask"]])
    n2 = S.op("dve", lambda e: e.tensor_tensor(bf_T[0:64, :, 0:64], bf_T[0:64, :, 0:64],
                                               negb[:, :].unsqueeze(1).to_broadcast([64, NH, 64]), ALU.add), deps=[n1, m_nb])
    m_bnew = split_hilo(bf_T[0:64, :, 0:64], bnew_h[:, :, :], bnew_l[:, :, :], bf_U[0:64, :, 0:64], [n2])
    m_bP = split_hilo(bf_P[:, :, :], bias_t["P"][0][:, :, :], bias_t["P"][1][:, :, :], bf_U[:, :, :], mP + m_bnew)
    x1 = S.op("dve", lambda e: e.tensor_scalar(bf_T[:, :, :], bf_P[:, :, :], par[:, 0:1], negpar[:, 0:1], ALU.mult, ALU.add),
              deps=mP + m_bnew + [m_np])
    m_bX1 = split_hilo(bf_T[:, :, :], bias_t["X1"][0][:, :, :], bias_t["X1"][1][:, :, :], bf_U[:, :, :], [x1] + m_bP)
    x2 = S.op("dve", lambda e: e.tensor_scalar(bf_T[:, :, :], bf_P[:, :, :], ompar[:, 0:1], None, ALU.mult),
              deps=m_bX1 + [m_op])
    m_bX2 = split_hilo(bf_T[:, :, :], bias_t["X2"][0][:, :, :], bias_t["X2"][1][:, :, :], bf_U[:, :, :], [x2])
    d1 = S.op("dve", lambda e: e.tensor_tensor(bf_T[:, :, :], bf_D[:, :, :],
                                               maskd[:, :].unsqueeze(1).to_broadcast([128, NH, 128]), ALU.mult), deps=m_bX2 + mD)
    d2 = S.op("dve", lambda e: e.tensor_tensor(bf_T[:, :, :], bf_T[:, :, :],
                                               negm[:, :].unsqueeze(1).to_broadcast([128, NH, 128]), ALU.add), deps=[d1, m_ng])
    m_bD = split_hilo(bf_T[:, :, :], bias_t["D"][0][:, :, :], bias_t["D"][1][:, :, :], bf_U[:, :, :], [d2])
    m_bias = m_bD + m_bX2 + m_bX1 + m_bP + m_bnew
    setup_ms = [m_identb, m_gpre, m_hnb, m_lam, m_ones, m_ones2, ld["convw"], ld["scT"]] + m_bias

    wsl_free = [None, None]
    wsl_n = [0]

    def load_slab(idx, extra_deps=()):
        b = wsl_n[0] % 2
        wsl_n[0] += 1
        m_ = S.dma("pool", f"wsl{b}", wsl[b][:, :, :], w_l[idx], deps=[wsl_free[b]] + list(extra_deps))
        return b, m_

    ev_rr = [0]

    def evac(dst, src, deps, func=None, scale=None):
        ev_rr[0] += 1
        if func is not None or ev_rr[0] % 2 == 0:
            f = func if func is not None else AF.Copy
            if scale is None:
                return S.op("act", (lambda e: e.activation(dst, src, f)), deps=deps)
            return S.op("act", (lambda e: e.activation(dst, src, f, scale=scale)), deps=deps)
        return S.op("dve", (lambda e: e.tensor_copy(dst, src)), deps=deps)

    pb_rr = [0]

    def next_bank():
        b = pb_rr[0] % 8
        pb_rr[0] += 1
        return b

    out_ms = []

    if _ph("B"):
        kvst_free = [None] * 3
        kv_n = [0]
        kf_n = [0]
        a_dep = a_done
        kt_dep = m_bias
        pend_tr = []

        def flush_tr():
            for (kfs, mf, fa, n, c0, h) in pend_tr:
                tb_ = next_bank()
                tps = bk(tb_)
                if n == 512:
                    blocks = [(k_ * 128, 128, k_) for k_ in range(4)]
                else:
                    blocks = [(16, 64, 0)]
                mt_ = None
                for bi, (o0, nt_, k_) in enumerate(blocks):
                    mt_ = S.op("pe", (lambda e, o0=o0, nt_=nt_, k_=k_, tps=tps, kfs=kfs: e.transpose(
                        tps[0:nt_, k_ * 128:(k_ + 1) * 128], kfs[:, o0:o0 + nt_], identf[:, :])),
                        deps=[mf, bank_free[tb_]] if bi == 0 else [], inc=(bi == len(blocks) - 1))
                kvst_free[fa] = mt_
                nb_ = len(blocks)
                rows_ = blocks[0][1]
                kts = kvst[2]
                me3 = S.op("dve" if fa == 0 else "act",
                           (lambda e, kts=kts, tps=tps, nb_=nb_, rows_=rows_, fa=fa: (
                               e.tensor_copy(kts[0:rows_, 0:nb_ * 128], tps[0:rows_, 0:nb_ * 128]) if fa == 0
                               else e.copy(kts[0:rows_, 0:nb_ * 128], tps[0:rows_, 0:nb_ * 128]))),
                           deps=[mt_, kvst_free[2]])
                bank_free[tb_] = me3
                if n == 512:
                    t0 = c0 // 128
                    dstd = ko.rearrange("(tb p) c -> p tb c", p=128)[:, t0:t0 + 4, h * 128:(h + 1) * 128]
                    srcd = kts[:, :].rearrange("p (tb c) -> p tb c", c=128)
                else:
                    dstd = kso[:, h * 128:(h + 1) * 128]
                    srcd = kts[0:64, 0:128]
                md = S.dma("sp", "kvst2", dstd, srcd, deps=[me3])
                kvst_free[2] = md
                out_ms.append(md)

            pend_tr.clear()

        b_order = [2, 3, 0, 1]
        pre = {b_order[0]: load_slab(b_order[0])}
        for oi, sl in enumerate(b_order):
            if oi + 1 < 4:
                pre[b_order[oi + 1]] = load_slab(b_order[oi + 1])
            wb, m_w = pre[sl]
            w = wsl[wb]
            last_use = []
            if sl < 2:
                groups = [(xnT, 0, 512, 0), (xnT, 512, 512, 512), (xnT, 1024, 80, 2048),
                          (xnT_oth, 0, 512, 1024), (xnT_oth, 512, 512, 1536)]
                for hh in range(4):
                    h = 4 * sl + hh
                    for (X, c0, n, dst0) in groups:
                        bnk = next_bank()
                        outp = bk(bnk)[:, 0:n]
                        for dc in range(16):
                            lhsT = w[:, dc, hh * 128:(hh + 1) * 128]
                            rhs = X[:, dc, c0:c0 + n]
                            mm = S.op("pe", (lambda e, outp=outp, lhsT=lhsT, rhs=rhs, dc=dc: e.matmul(
                                outp, lhsT, rhs, start=(dc == 0), stop=(dc == 15))),
                                deps=[m_w, bank_free[bnk]] + a_dep if dc == 0 else [], inc=(dc == 15))
                        flush_tr()
                        mev = evac(KT[:, h, dst0:dst0 + n], outp, [mm] + kt_dep)
                        bank_free[bnk] = mev
                        last_use = [mm]
                        if X is xnT:
                            fa = kf_n[0] % 2
                            kf_n[0] += 1
                            kfs = kvst[fa]
                            mf = S.op("act" if fa == 0 else "dve",
                                      (lambda e, kfs=kfs, outp=outp, n=n, fa=fa: (e.copy(kfs[:, 0:n], outp) if fa == 0
                                                                              else e.tensor_copy(kfs[:, 0:n], outp))),
                                      deps=[mm, kvst_free[fa], mev])
                            bank_free[bnk] = [mev, mf]
                            pend_tr.append((kfs, mf, fa, n, c0, h))
            else:
                flush_tr()
                vs = sl - 2
                vt = [("own", i) for i in range(8)] + [("oth", i) for i in range(8)] + [("smp", 0)]
                for (kind, i) in vt:
                    rows = 64 if kind == "smp" else 128
                    if kind == "own":
                        X, c0 = xnT, i * 128
                    elif kind == "oth":
                        X, c0 = xnT_oth, i * 128
                    else:
                        X, c0 = xnT, CS0
                    bnk = next_bank()
                    outp = bk(bnk)[0:rows, :]
                    for dc in range(16):
                        lhsT = X[:, dc, c0:c0 + rows]
                        rhs = w[:, dc, :]
                        mm = S.op("pe", (lambda e, outp=outp, lhsT=lhsT, rhs=rhs, dc=dc: e.matmul(
                            outp, lhsT, rhs, start=(dc == 0), stop=(dc == 15))),
                            deps=[m_w, bank_free[bnk]] + a_dep if dc == 0 else [], inc=(dc == 15))
                    last_use = [mm]
                    if kind == "oth":
                        dstv = Vaug[:, 8 + i, 4 * vs:4 * vs + 4, 0:128]
                        mev = evac(dstv, outp.rearrange("p (h e) -> p h e", e=128), [mm])
                        bank_free[bnk] = mev
                    else:
                        sb_ = kv_n[0] % 3
                        kv_n[0] += 1
                        st = kvst[sb_]
                        mev = evac(st[0:rows, :], outp, [mm, kvst_free[sb_]])
                        bank_free[bnk] = mev
                        if kind == "own":
                            dstv = Vaug[:, i, 4 * vs:4 * vs + 4, 0:128]
                            dstd = vo[i * 128:(i + 1) * 128, vs * 512:(vs + 1) * 512]
                        else:
                            dstv = vnew[:, 4 * vs:4 * vs + 4, 0:128]
                            dstd = vso[:, vs * 512:(vs + 1) * 512]
                        srcv = st[0:rows, :].rearrange("p (h e) -> p h e", e=128)
                        mp = S.op("pool", (lambda e, dstv=dstv, srcv=srcv: e.tensor_copy(dstv, srcv)), deps=[mev])
                        md = S.dma("sp", f"kvst{sb_}", dstd, st[0:rows, :], deps=[mev])
                        kvst_free[sb_] = [mp, md]
                        out_ms.append(md)
            wsl_free[wb] = last_use
        b_done = [S.last["act"], S.last["dve"], S.last["pool"], S.last["pe"]]

    if _ph("C1"):
        c1_dep = [S.last["pe"]] + b_done
        pre = {}
        pre[4] = load_slab(4)
        for sl in range(4, 8):
            if sl + 1 < 8:
                pre[sl + 1] = load_slab(sl + 1)
            wb, m_w = pre[sl]
            w = wsl[wb]
            last_use = []
            if sl < 6:
                for hh in range(4):
                    h = 4 * (sl - 4) + hh
                    for (c0, n) in ((0, 368), (368, 368), (736, 368)):
                        bnk = next_bank()
                        outp = bk(bnk)[:, 0:n]
                        for dc in range(16):
                            lhsT = w[:, dc, hh * 128:(hh + 1) * 128]
                            rhs = xnT[:, dc, c0:c0 + n]
                            mm = S.op("pe", (lambda e, outp=outp, lhsT=lhsT, rhs=rhs, dc=dc: e.matmul(
                                outp, lhsT, rhs, start=(dc == 0), stop=(dc == 15))),
                                deps=[m_w, bank_free[bnk]] if dc == 0 else [], inc=(dc == 15))
                        mev = evac(QT[:, h, c0:c0 + n], outp, [mm] + c1_dep)
                        bank_free[bnk] = mev
                        last_use = [mm]
            else:
                zs = sl - 6
                jobs = [("own", i) for i in range(8)] + [("smp", s) for s in range(4)]
                for (kind, i) in jobs:
                    rows = 128 if kind == "own" else 16
                    c0 = i * 128 if kind == "own" else CS0 + 16 * i
                    bnk = next_bank()
                    outp = bk(bnk)[0:rows, :]
                    for dc in range(16):
                        lhsT = xnT[:, dc, c0:c0 + rows]
                        rhs = w[:, dc, :]
                        mm = S.op("pe", (lambda e, outp=outp, lhsT=lhsT, rhs=rhs, dc=dc: e.matmul(
                            outp, lhsT, rhs, start=(dc == 0), stop=(dc == 15))),
                            deps=[m_w, bank_free[bnk]] if dc == 0 else [], inc=(dc == 15))
                    last_use = [mm]
                    sb_ = kv_n[0] % 3
                    kv_n[0] += 1
                    st = kvst[sb_]
                    stv = st[0:rows, :]
                    mev = S.op("act", (lambda e, stv=stv, outp=outp: e.activation(stv, outp, AF.Silu)),
                               deps=[mm, kvst_free[sb_]])
                    bank_free[bnk] = mev
                    if kind == "own":
                        dstz = zas[:, i, zs * 512:(zs + 1) * 512].rearrange("p (h e) -> p h e", e=128)
                    else:
                        dstz = zass[:, i, zs * 512:(zs + 1) * 512].rearrange("p (h e) -> p h e", e=128)
                    srcz = stv.rearrange("p (h e) -> p h e", e=128)
                    hb = hnb[0:rows, :].unsqueeze(1).to_broadcast([rows, 4, 128])
                    mp = S.op("pool", (lambda e, dstz=dstz, srcz=srcz, hb=hb: e.tensor_tensor(dstz, srcz, hb, ALU.mult)),
                              deps=[mev, m_hnb] + c1_dep)
                    kvst_free[sb_] = mp
            wsl_free[wb] = last_use
        c1_done = [S.last["act"], S.last["dve"], S.last["pool"], S.last["pe"]]

    if _ph("D"):
        d_dep = c1_done + setup_ms + [m for m in out_ms[-3:]]
        d_dep = d_dep + [x for x in (wsl_free[0], wsl_free[1]) if x is not None]
        for x in kvst_free:
            if x is not None:
                d_dep.append(x)
        S.wait("pe", d_dep)
        S.wait("act", d_dep)
        S.wait("dve", d_dep)
        S.wait("pool", d_dep)
        S_SETS = [(0, 1), (2, 3)]
        m_v0one = S.op("pool", lambda e: e.memset(V0s[:, :, :, 128:129], 1.0))
        m_k0s = S.dma("pool", "k0s", K0s[:, :, :], ck[0, 0])
        m_v0s = [S.dma("pool", f"v0s{hh_}", V0s[:, :, hh_, 0:128], cv[0, 0][:, :, hh_ * 128:(hh_ + 1) * 128], deps=[m_v0one])
                 for hh_ in range(2)]
        s_n = [0]
        pt_free = [None] * 3
        ep_free = [None, None]
        ep_n = [0]
        est_n = [0]

        pending = []
        epj_prev = [None]
        pendingB = []

        def flush_B(n, force=False):
            keep = []
            for (due, fn) in pendingB:
                if force or due <= n:
                    fn()
                else:
                    keep.append((due, fn))
            pendingB[:] = keep

        def flush_pending(n, force=False):
            keep = []
            for (due, fn) in pending:
                if force or due <= n:
                    fn()
                else:
                    keep.append((due, fn))
            pending[:] = keep

        def epilogue(rows, O1, O2, zsrc, dst_cols, h, par_b, o_banks, o_ready, n_step):
            k = est_n[0] % 8
            est_n[0] += 1
            st = est[0:rows, k, :]
            pb_ = par_b
            t_ = ep_t[pb_][0:rows, :]
            d_ = ep_d[pb_][0:rows, :]
            oz_ = ep_oz[pb_][0:rows, :]
            deps0 = [o_ready]
            a1 = S.op("dve", (lambda e: e.reciprocal(st[:, 0:1], O1[:, 128:129])), deps=deps0)
            a2 = S.op("dve", (lambda e: e.reciprocal(st[:, 1:2], O2[:, 128:129])), deps=deps0)
            a3 = S.op("dve", (lambda e: e.tensor_tensor(st[:, 2:3], st[:, 1:2], lams[0:rows, 5:6], ALU.mult)),
                      deps=[a2, m_lam])
            a4 = S.op("dve", (lambda e: e.tensor_scalar(t_, O2[:, 0:128], st[:, 2:3], None, ALU.mult)),
                      deps=[a3, ep_free[pb_]])
            a5 = S.op("dve", (lambda e: e.scalar_tensor_tensor(d_, O1[:, 0:128], st[:, 0:1], t_, ALU.mult, ALU.add)),
                      deps=[a1, a4])
            bank_free[o_banks[1]] = a5
            bank_free[o_banks[0]] = ("never", 1 << 30)
            ep_free[pb_] = ("never", 1 << 30)
            hold = {}

            def partB():
                a6 = S.op("act", (lambda e: e.activation(ep_junk[0:rows, :], d_, AF.Square, accum_out=st[:, 3:4])),
                          deps=[a5, epj_prev[0]])
                epj_prev[0] = a6
                a7a = S.op("act", (lambda e: e.activation(st[:, 4:5], st[:, 3:4], AF.Ln, bias=eps128[0:rows, 0:1])),
                           deps=[a6, m_eps128])
                a7 = S.op("act", (lambda e: e.activation(st[:, 4:5], st[:, 4:5], AF.Exp, scale=-0.5)), deps=[a7a])
                hold["a8"] = S.op("dve", (lambda e: e.scalar_tensor_tensor(oz_, d_, st[:, 4:5], zsrc, ALU.mult, ALU.mult)),
                                  deps=[a7, a6])

            def partC():
                tp = bk16(o_banks[0])[:, 0:rows]
                a9 = S.op("pe", (lambda e: e.transpose(tp, oz_, identb[0:rows, 0:rows])), deps=[hold["a8"], a5])
                a10 = S.op("act", (lambda e: e.copy(QT[:, h, dst_cols[0]:dst_cols[1]], tp)), deps=[a9])
                bank_free[o_banks[0]] = a10
                ep_free[pb_] = a9
            pendingB.append((n_step + 1, partB))
            pending.append((n_step + 2, partC))

        steps = []
        for i in range(D_SLOTS):
            far = [("own", j, None) for j in range(i)] + [("oth", j, None) for j in range(max(0, i - 1))]
            near = ([("oth", i - 1, "X2")] if i >= 1 else []) + [("oth", i, "X1"), ("own", i, "D")]
            tl = far + near
            n_t = len(tl)
            groups = []
            idx = n_t % 4
            if idx:
                groups.append(tl[0:idx])
            while idx < n_t:
                groups.append(tl[idx:idx + 4])
                idx += 4
            for h in range(D_HEADS):
                pe_par = ep_n[0] % 2
                ep_n[0] += 1
                ob = (4, 5) if pe_par == 0 else (6, 7)
                for gi, grp in enumerate(groups):
                    steps.append(dict(i=i, h=h, gi=gi, grp=grp, ng=len(groups), ob=ob, par=pe_par))

        def stage_S(st, n):
            sset = S_SETS[n % 2]
            grp = st["grp"]
            nt = len(grp)
            i, h = st["i"], st["h"]
            Sm = [bk(sset[m_])[:, 0:nt * 128].rearrange("p (t q) -> p t q", q=128) for m_ in range(2)]
            first = True
            mm = None
            for ti, (kind, j, mult) in enumerate(grp):
                kc = (j * 128) if kind == "own" else (1024 + j * 128)
                for m_ in range(2):
                    lhsT = KT[64 * m_:64 * m_ + 64, h, kc:kc + 128]
                    rhs = QT[64 * m_:64 * m_ + 64, h, i * 128:(i + 1) * 128]
                    outp = Sm[m_][:, ti, :]
                    mm = S.op("pe", (lambda e, outp=outp, lhsT=lhsT, rhs=rhs, near_=(mult is not None): e.matmul(
                        outp, lhsT, rhs, start=True, stop=(not near_))),
                        deps=[bank_free[sset[0]], bank_free[sset[1]]] if first else [],
                        inc=(ti == nt - 1 and m_ == 1 and mult is None))
                    first = False
                if mult is not None:
                    for m_ in range(2):
                        outp = Sm[m_][:, ti, :]
                        for part in range(2):
                            rhs = bias_t[mult][part][:, h, :]
                            mm = S.op("pe", (lambda e, outp=outp, rhs=rhs, part=part: e.matmul(
                                outp, identb[:, :], rhs, start=False, stop=(part == 1))),
                                deps=[], inc=(ti == nt - 1 and m_ == 1 and part == 1))
            st["Sm"] = Sm
            st["mm"] = mm
            st["sset"] = sset

        def stage_X(st, n):
            grp = st["grp"]
            nt = len(grp)
            h = st["h"]
            pbuf = n % 3
            P_ = Pt[pbuf]
            Sm, mm, sset = st["Sm"], st["mm"], st["sset"]
            mexs = []
            for m_ in range(2):
                mexs.append(S.op("act", (lambda e, P_=P_, src=Sm[m_], m_=m_, nt=nt: e.activation(
                    P_[:, m_, 0:nt, :], src, AF.Exp, scale=0.125)), deps=[mm, pt_free[pbuf]]))
                bank_free[sset[m_]] = mexs[-1]
            p_ready = mexs
            st["p_ready"] = p_ready
            st["P"] = P_
            st["pbuf"] = pbuf

        def stage_AV(st, n):
            grp = st["grp"]
            nt = len(grp)
            i, h, gi, ng, ob = st["i"], st["h"], st["gi"], st["ng"], st["ob"]
            P_ = st["P"]
            O1 = bk(ob[0])[:, 0:129]
            O2 = bk(ob[1])[:, 0:129]
            Os = (O1, O2)
            mm2 = None
            for ti, (kind, j, mult) in enumerate(grp):
                vt_ = j if kind == "own" else 8 + j
                for m_ in range(2):
                    lhsT = P_[:, m_, ti, :]
                    rhs = Vaug[:, vt_, h, :]
                    isfirst = (gi == 0 and ti == 0)
                    islast = (gi == ng - 1 and ti == nt - 1)
                    dps = list(st["p_ready"]) if (ti == 0 and m_ == 0) else []
                    if isfirst:
                        dps.append(bank_free[ob[m_]])
                    outp = Os[m_]
                    mm2 = S.op("pe", (lambda e, outp=outp, lhsT=lhsT, rhs=rhs, isfirst=isfirst, islast=islast:
                                      e.matmul(outp, lhsT, rhs, start=isfirst, stop=islast)),
                               deps=dps, inc=(ti == nt - 1 and m_ == 1))
            pt_free[st["pbuf"]] = mm2
            if gi == ng - 1 and D_EPI:
                zsrc = zas[:, i, h * 128:(h + 1) * 128]
                epilogue(128, O1, O2, zsrc, (i * 128, (i + 1) * 128), h, st["par"], ob, mm2, n)

        if steps:
            stage_S(steps[0], 0)
        for n, st in enumerate(steps):
            stage_X(st, n)
            flush_B(n)
            if n + 1 < len(steps):
                stage_S(steps[n + 1], n + 1)
            flush_pending(n)
            stage_AV(st, n)
        flush_B(0, force=True)
        flush_pending(0, force=True)

        pool_ones = []
        ksh_free = [None, None]
        vsh_free = [None, None]
        ktsh_free = [None]
        pts_free = [None, None]
        ptn_used = [None]
        jobs = [dict(h=2 * hp + hh, s=s_, hh=hh) for hp in range(4 if D_SAMPLE else 0) for s_ in range(4) for hh in range(2)]
        pinfo = {}
        vf2_free = [None, None]
        ksh2_free = [None, None]
        vsh2_free = [None, None]

        def pair_dma(p):
            hp, s_ = p // 4, p % 4
            b = p % 2
            if p == 0:
                pinfo[p] = [m_k0s, None, m_k0s, None]
                return
            mk = S.dma("pool", f"ksh2{b}", Ksh2[b][:, :, :], ck[s_, hp], deps=[ksh2_free[b]] + sample_start)
            mvf = S.dma("sp", f"vf2{b}", Vf2[b][:, :, :], cv[s_, hp], deps=[vf2_free[b]] + sample_start)
            pinfo[p] = [mk, mvf, mk, None]

        def pair_cast(p):
            b = p % 2
            pass
        head_ptn = {}

        def stage_N(h):
            mm = None
            for m_ in range(2):
                lhsT = KT[64 * m_:64 * m_ + 64, h, 2064:2128]
                rhs = QT[64 * m_:64 * m_ + 64, h, CS0:CS0 + 64]
                outp = bk(m_)[0:64, 0:64]
                mm = S.op("pe", (lambda e, outp=outp, lhsT=lhsT, rhs=rhs: e.matmul(outp, lhsT, rhs, start=True, stop=False)),
                          deps=[bank_free[0], bank_free[1]] if m_ == 0 else [], inc=False)
            for m_ in range(2):
                outp = bk(m_)[0:64, 0:64]
                for part, bt in enumerate((bnew_h, bnew_l)):
                    rhs = bt[:, h, :]
                    mm = S.op("pe", (lambda e, outp=outp, rhs=rhs, part=part: e.matmul(
                        outp, identb[0:64, 0:64], rhs, start=False, stop=(part == 1))),
                        deps=[], inc=(m_ == 1 and part == 1))
            mexn = []
            for m_ in range(2):
                mexn.append(S.op("act", (lambda e, m_=m_, h=h: e.activation(Ptn_all[:, h, m_, :], bk(m_)[0:64, 0:64], AF.Exp, scale=0.125)),
                                 deps=[mm]))
                bank_free[m_] = mexn[-1]
            head_ptn[h] = mexn

        def stage_T(jb, n):
            h, s_ = jb["h"], jb["s"]
            p = n // 2
            hh = n % 2
            b = p % 2
            if hh == 0:
                pair_cast(p)
                if p + 1 < len(jobs) // 2:
                    pair_dma(p + 1)
            if p == 0:
                K_ = K0s[:, :, hh * 128:(hh + 1) * 128]
                V_ = V0s[:, :, hh, :]
            else:
                K_ = Ksh2[b][:, :, hh * 128:(hh + 1) * 128]
                V_ = Vsh2[b][:, :, hh, :]
            KT_ = KTsh[0]
            mk = pinfo[p][2]
            mvf = pinfo[p][1]
            mts = []
            mt = None
            for half in range(2):
                bnk = half
                pv = bk16(bnk).rearrange("p (c t) -> p c t", t=128)
                for cc in range(8):
                    kt = half * 8 + cc
                    outp = pv[:, cc, :]
                    inp = K_[:, kt, :]
                    mt = S.op("pe", (lambda e, outp=outp, inp=inp: e.transpose(outp, inp, identb[:, :])),
                              deps=[mk, bank_free[bnk]] if cc == 0 else [], inc=(cc == 7))
                dst = KT_[:, half * 1024:(half + 1) * 1024]
                srcp = bk16(bnk)
                if half == 0:
                    mev = S.op("act", (lambda e, dst=dst, srcp=srcp: e.copy(dst, srcp)), deps=[mt, ktsh_free[0]])
                else:
                    mev = S.op("dve", (lambda e, dst=dst, srcp=srcp: e.tensor_copy(dst, srcp)), deps=[mt, ktsh_free[0]])
                bank_free[bnk] = mev
                mts.append(mev)
            if hh == 1 and p != 0:
                ksh2_free[b] = mt
            if p == 0:
                mv = m_v0s
            else:
                mv = S.op("dve", (lambda e: e.tensor_copy(Vsh2[b][:, :, hh, 0:128], Vf2[b][:, :, hh * 128:(hh + 1) * 128])),
                          deps=[mvf, vsh2_free[b], pool_ones[b]])
                if hh == 1:
                    vf2_free[b] = mv
            jb.update(b=b, V=V_, KT=KT_, mts=mts, mv=mv, hh=hh)

        def stage_SS(jb, n):
            h, s_ = jb["h"], jb["s"]
            KT_ = jb["KT"]
            Sm = [bk(2 + m_)[:, 0:256].rearrange("p (t q) -> p t q", q=16) for m_ in range(2)]
            q0 = CS0 + 16 * s_
            mm = None
            for kt in range(16):
                for m_ in range(2):
                    lhsT = KT_[64 * m_:64 * m_ + 64, kt * 128:(kt + 1) * 128]
                    rhs = QT[64 * m_:64 * m_ + 64, h, q0:q0 + 16]
                    outp = Sm[m_][:, kt, :]
                    mm = S.op("pe", (lambda e, outp=outp, lhsT=lhsT, rhs=rhs, kt=kt: e.matmul(outp, lhsT, rhs, start=True, stop=(kt != 15))),
                              deps=jb["mts"] + [bank_free[2], bank_free[3]] if (kt == 0 and m_ == 0) else [],
                              inc=(kt == 15 and m_ == 1))
            ktsh_free[0] = mm
            for m_ in range(2):
                outp = Sm[m_][:, 15, :]
                for part in range(2):
                    rhs = bias_t["P"][part][:, h, 0:16]
                    mm = S.op("pe", (lambda e, outp=outp, rhs=rhs, part=part: e.matmul(
                        outp, identb[:, :], rhs, start=False, stop=(part == 1))),
                        deps=[], inc=(m_ == 1 and part == 1))
            jb.update(Sm=Sm, mm=mm, q0=q0)

        def stage_SX(jb, n):
            pb = n % 2
            P_ = Pts[pb]
            Sm, mm = jb["Sm"], jb["mm"]
            mexa = []
            for m_ in range(2):
                mexa.append(S.op("act", (lambda e, P_=P_, src=Sm[m_], m_=m_: e.activation(
                    P_[:, m_, :, :], src, AF.Exp, scale=0.125)), deps=[mm, pts_free[pb]]))
                bank_free[2 + m_] = mexa[-1]
            jb.update(P=P_, mexa=mexa, pb=pb)

        def stage_SAV(jb, n):
            h, s_ = jb["h"], jb["s"]
            b = jb["b"]
            P_, V_ = jb["P"], jb["V"]
            pe_par = n % 2
            ob = (4, 5) if pe_par == 0 else (6, 7)
            O1 = bk(ob[0])[0:16, 0:129]
            O2 = bk(ob[1])[0:16, 0:129]
            Os = (O1, O2)
            mm2 = None
            for kt in range(16):
                for m_ in range(2):
                    lhsT = P_[:, m_, kt, :]
                    rhs = V_[:, kt, :]
                    dps = []
                    if kt == 0:
                        dps = [bank_free[ob[m_]]]
                        if m_ == 0:
                            dps += jb["mexa"] + [jb["mv"]] + head_ptn[h]
                    outp = Os[m_]
                    mm2 = S.op("pe", (lambda e, outp=outp, lhsT=lhsT, rhs=rhs, kt=kt: e.matmul(
                        outp, lhsT, rhs, start=(kt == 0), stop=False)), deps=dps, inc=False)
            for m_ in range(2):
                lhsT = Ptn_all[0:64, h, m_, 16 * s_:16 * s_ + 16]
                rhs = vnew[0:64, h, :]
                outp = Os[m_]
                mm2 = S.op("pe", (lambda e, outp=outp, lhsT=lhsT, rhs=rhs: e.matmul(outp, lhsT, rhs, start=False, stop=True)),
                           deps=[], inc=(m_ == 1))
            pts_free[jb["pb"]] = mm2
            if jb["hh"] == 1:
                vsh2_free[b] = mm2
            ptn_used[0] = mm2
            zsrc = zass[0:16, s_, h * 128:(h + 1) * 128]
            epilogue(16, O1, O2, zsrc, (jb["q0"], jb["q0"] + 16), h, pe_par, ob, mm2, n)

        kf_free = [None, None]
        vf_free = [None, None]
        sample_start = []
        if jobs:
            for h_ in range(NH):
                stage_N(h_)
            sample_start.append(S.last["pe"])
            for v_ in Vsh2:
                pool_ones.append(S.op("pool", (lambda e, v_=v_: e.memset(v_[:, :, :, 128:129], 1.0)), deps=sample_start))
            pair_dma(0)
            stage_T(jobs[0], 0)
            stage_SS(jobs[0], 0)
        for n, jb in enumerate(jobs):
            stage_SX(jb, n)
            flush_B(n)
            if n + 1 < len(jobs):
                nj = jobs[n + 1]
                if nj["h"] != jb["h"]:
                    pass
                stage_T(nj, n + 1)
            flush_pending(n)
            stage_SAV(jb, n)
            if n + 1 < len(jobs):
                nj = jobs[n + 1]
                stage_SS(nj, n + 1)
        flush_B(0, force=True)
        flush_pending(0, force=True)
        d_done = [S.last["act"], S.last["dve"], S.last["pool"], S.last["pe"]]

    if _ph("C2"):
        for e_ in ("pe", "act", "dve", "pool"):
            S.wait(e_, d_done)
        m_rz = S.op("pool", lambda e: e.memset(rT[:, :, 1024:1040], 0.0))
        wsl_free = [None, None]
        pre = {}
        pre[8] = load_slab(8, extra_deps=d_done)
        groups = [(0, 512), (512, 512), (1024, 80)]
        u_free = [None, None]
        g_free = [None, None]
        h_free = [None, None]
        z_free = [None, None]
        hz_n = [0]
        y_free = [None]
        wpa_free = [None, None]
        pre_e = {}

        def load_e(g):
            b_, m1 = load_slab(16 + g, extra_deps=d_done)
            pb2 = g % 2
            m2 = S.dma("pool", f"wpa{pb2}", wpa[pb2][:, :, :], wpa_l[g], deps=[wpa_free[pb2]] + d_done)
            m3 = S.dma("pool", f"wpc{pb2}", wpc[pb2][:, :, :], wpc_l[g], deps=[wpa_free[pb2]] + d_done)
            return b_, [m1, m2, m3]

        for c in range(8):
            sl = 8 + c
            if c + 1 < 8:
                pre[sl + 1] = load_slab(sl + 1, extra_deps=d_done)
            elif _ph("E"):
                pre_e[0] = load_e(0)
            wb, m_w = pre[sl]
            w = wsl[wb]
            ub = c % 2
            U = ubuf[ub]
            G = gbuf[ub]
            Us = U[:, 1040:1112].rearrange("p (s t) -> p s t", t=18)
            m_st = S.op("pool", (lambda e, Us=Us, c=c: e.tensor_copy(Us[:, :, 0:2], scT[:, c, :, :])),
                        deps=[u_free[ub], ld["scT"]])
            u_ms = [m_st]
            g_ms = []
            last_mm = None
            for (c0, n) in groups:
                bks = [next_bank() for _ in range(4)]
                outs = [bk(b_)[:, 0:n] for b_ in bks]
                mms = []
                for part in range(4):
                    for dc in range(16):
                        lhsT = w[:, dc, part * 128:(part + 1) * 128]
                        rhs = xnT[:, dc, c0:c0 + n]
                        outp = outs[part]
                        mm = S.op("pe", (lambda e, outp=outp, lhsT=lhsT, rhs=rhs, dc=dc: e.matmul(
                            outp, lhsT, rhs, start=(dc == 0), stop=(dc == 15))),
                            deps=[m_w, bank_free[bks[part]]] if dc == 0 else [], inc=(dc == 15))
                    mms.append(mm)
                last_mm = mm
                hb_ = hz_n[0] % 2
                hz_n[0] += 1
                Hs = c2h[hb_][:, 0:n]
                Zs = c2z[hb_][:, 0:n]
                mh = S.op("act", (lambda e, Hs=Hs, src=outs[2]: e.copy(Hs, src)), deps=[mms[2], h_free[hb_]])
                bank_free[bks[2]] = mh
                mz = S.op("act", (lambda e, Zs=Zs, src=outs[3]: e.activation(Zs, src, AF.Silu)), deps=[mms[3], z_free[hb_]])
                bank_free[bks[3]] = mz
                if c0 < 1024:
                    b0 = c0 // 128
                    Uv = U[:, 0:1040].rearrange("p (b t) -> p b t", t=130)[:, b0:b0 + 4, 2:130]
                    mu = S.op("dve", (lambda e, Uv=Uv, src=outs[1], Hs=Hs: e.tensor_tensor(
                        Uv, src.rearrange("p (b t) -> p b t", t=128), Hs.rearrange("p (b t) -> p b t", t=128), ALU.mult)),
                        deps=[mms[1], mh, u_free[ub]])
                    u_ms.append(mu)
                    lastu = mu
                else:
                    Uh = U[:, 0:1040].rearrange("p (b t) -> p b t", t=130)[:, :, 0:2]
                    mu1 = S.op("dve", (lambda e, Uh=Uh, src=outs[1], Hs=Hs: e.tensor_tensor(
                        Uh, src[:, 0:16].rearrange("p (b t) -> p b t", t=2), Hs[:, 0:16].rearrange("p (b t) -> p b t", t=2), ALU.mult)),
                        deps=[mms[1], mh, u_free[ub]])
                    Usm = Us[:, :, 2:18]
                    mu2 = S.op("dve", (lambda e, Usm=Usm, src=outs[1], Hs=Hs: e.tensor_tensor(
                        Usm, src[:, 16:80].rearrange("p (s t) -> p s t", t=16), Hs[:, 16:80].rearrange("p (s t) -> p s t", t=16), ALU.mult)),
                        deps=[mms[1], mh, u_free[ub]])
                    u_ms += [mu1, mu2]
                    lastu = mu2
                bank_free[bks[1]] = lastu
                h_free[hb_] = lastu
                mg = S.op("dve", (lambda e, G=G, src=outs[0], Zs=Zs, c0=c0, n=n: e.tensor_tensor(
                    G[:, c0:c0 + n], src, Zs, ALU.mult)), deps=[mms[0], mz, g_free[ub]])
                bank_free[bks[0]] = mg
                z_free[hb_] = mg
                g_ms.append(mg)
            wsl_free[wb] = last_mm
            Ub = U[:, 0:1040].rearrange("p (b t) -> p b t", t=130)
            Yp = ybuf[:, 0:1024].rearrange("p (b t) -> p b t", t=128)
            Ys = ybuf[:, CS0:CS0 + 64].rearrange("p (s t) -> p s t", t=16)
            w0 = convw[:, c, 0:1]
            w1 = convw[:, c, 1:2]
            w2 = convw[:, c, 2:3]
            ydeps = u_ms + [y_free[0], ld["convw"]]
            y1 = S.op("dve", (lambda e, Ub=Ub, Yp=Yp, w2=w2: e.tensor_scalar(Yp, Ub[:, :, 2:130], w2, None, ALU.mult)), deps=ydeps)
            y1s = S.op("dve", (lambda e, Us=Us, Ys=Ys, w2=w2: e.tensor_scalar(Ys, Us[:, :, 2:18], w2, None, ALU.mult)), deps=ydeps)
            y2 = S.op("dve", (lambda e, Ub=Ub, Yp=Yp, w1=w1: e.scalar_tensor_tensor(Yp, Ub[:, :, 1:129], w1, Yp, ALU.mult, ALU.add)), deps=[y1])
            y2s = S.op("dve", (lambda e, Us=Us, Ys=Ys, w1=w1: e.scalar_tensor_tensor(Ys, Us[:, :, 1:17], w1, Ys, ALU.mult, ALU.add)), deps=[y1s])
            y3 = S.op("dve", (lambda e, Ub=Ub, Yp=Yp, w0=w0: e.scalar_tensor_tensor(Yp, Ub[:, :, 0:128], w0, Yp, ALU.mult, ALU.add)), deps=[y2])
            y3s = S.op("dve", (lambda e, Us=Us, Ys=Ys, w0=w0: e.scalar_tensor_tensor(Ys, Us[:, :, 0:16], w0, Ys, ALU.mult, ALU.add)), deps=[y2s])
            n1 = S.op("pool", (lambda e, Ub=Ub, c=c: e.tensor_copy(ncv[:, c, 0:1, :], Ub[:, 7:8, 128:130])), deps=u_ms)
            n2 = S.op("pool", (lambda e, Us=Us, c=c: e.tensor_copy(ncv[:, c, 1:5, :], Us[:, :, 16:18])), deps=u_ms)
            u_free[ub] = [y3, y3s, n1, n2]
            r1 = S.op("dve", (lambda e, G=G, c=c: e.tensor_tensor(rT[:, c, 0:1024], G[:, 0:1024], ybuf[:, 0:1024], ALU.mult)),
                      deps=g_ms + [y3])
            r2 = S.op("dve", (lambda e, G=G, c=c: e.tensor_tensor(rT[:, c, CS0:CS0 + 64], G[:, CS0:CS0 + 64], ybuf[:, CS0:CS0 + 64], ALU.mult)),
                      deps=g_ms + [y3s])
            g_free[ub] = [r1, r2]
            y_free[0] = [r1, r2]
        for r_ in range(5):
            for j_ in range(2):
                m_cv = S.dma("sp", "cvo", cvo[r_, j_].rearrange("(c p) -> p c", p=128), ncv[:, :, r_, j_],
                             deps=[S.last["pool"]], allow_slow_non_contiguous=True)
        out_ms.append(m_cv)
        c2_done = [S.last["act"], S.last["dve"], S.last["pool"], S.last["pe"]]

    pre_f = {}

    def load_wo(cg):
        b_ = wsl_n[0] % 2
        wsl_n[0] += 1
        m_ = S.dma("pool", f"wsl{b_}", wsl[b_][:, :, :], wo_l[:, :, cg * 512:(cg + 1) * 512], deps=[wsl_free[b_]])
        return b_, m_

    if _ph("E"):
        for e_ in ("pe", "act", "dve", "pool"):
            S.wait(e_, c2_done)
        e_groups = [(0, 368), (368, 368), (736, 368)]
        sg_free = [None] * 4
        t_free = [None] * 4
        en = [0]
        pre = dict(pre_e)
        for g in range(8):
            if g + 1 < 8:
                pre[g + 1] = load_e(g + 1)
            elif _ph("F"):
                pre_f[0] = load_wo(0)
            wb, m_w = pre[g]
            w = wsl[wb]
            A_ = wpa[g % 2]
            C_ = wpc[g % 2]
            last_mm = None
            for jj in range(2):
                j = 2 * g + jj
                for (c0, n) in e_groups:
                    par_ = en[0] % 2
                    en[0] += 1
                    bks = [4 * par_ + k for k in range(4)]
                    outs = [bk(b_)[:, 0:n] for b_ in bks]
                    mms = []
                    for ec in range(8):
                        mm = S.op("pe", (lambda e, outp=outs[0], lhsT=A_[:, ec, jj * 128:(jj + 1) * 128], rhs=QT[:, ec, c0:c0 + n], ec=ec:
                                         e.matmul(outp, lhsT, rhs, start=(ec == 0), stop=(ec == 7))),
                                  deps=m_w + [bank_free[bks[0]]] if ec == 0 else [], inc=(ec == 7))
                    mms.append(mm)
                    for ec in range(8):
                        mm = S.op("pe", (lambda e, outp=outs[1], lhsT=C_[:, ec, jj * 128:(jj + 1) * 128], rhs=rT[:, ec, c0:c0 + n], ec=ec:
                                         e.matmul(outp, lhsT, rhs, start=(ec == 0), stop=(ec == 7))),
                                  deps=m_w + [bank_free[bks[1]]] if ec == 0 else [], inc=(ec == 7))
                    mms.append(mm)
                    for gi in range(2):
                        for dc in range(16):
                            mm = S.op("pe", (lambda e, outp=outs[2 + gi], lhsT=w[:, dc, gi * 256 + jj * 128:gi * 256 + (jj + 1) * 128],
                                             rhs=xnT[:, dc, c0:c0 + n], dc=dc:
                                             e.matmul(outp, lhsT, rhs, start=(dc == 0), stop=(dc == 15))),
                                      deps=m_w + [bank_free[bks[2 + gi]]] if dc == 0 else [], inc=(dc == 15))
                        mms.append(mm)
                    last_mm = mm
                    sb0 = 2 * par_
                    sga = e_sg[sb0][:, 0:n]
                    sgc = e_sg[sb0 + 1][:, 0:n]
                    t1 = e_t[sb0][:, 0:n]
                    t2 = e_t[sb0 + 1][:, 0:n]
                    ms1 = S.op("act", (lambda e, sga=sga, src=outs[2]: e.activation(sga, src, AF.Sigmoid)), deps=[mms[2], sg_free[sb0]])
                    bank_free[bks[2]] = ms1
                    ms2 = S.op("act", (lambda e, sgc=sgc, src=outs[3]: e.activation(sgc, src, AF.Sigmoid)), deps=[mms[3], sg_free[sb0 + 1]])
                    bank_free[bks[3]] = ms2
                    mt1 = S.op("dve", (lambda e, t1=t1, src=outs[0], sga=sga: e.tensor_tensor(t1, src, sga, ALU.mult)),
                               deps=[mms[0], ms1, t_free[sb0]])
                    bank_free[bks[0]] = mt1
                    sg_free[sb0] = mt1
                    mt2 = S.op("dve", (lambda e, t2=t2, src=outs[1], sgc=sgc: e.tensor_tensor(t2, src, sgc, ALU.mult)),
                               deps=[mms[1], ms2, t_free[sb0 + 1]])
                    bank_free[bks[1]] = mt2
                    sg_free[sb0 + 1] = mt2
                    mad = S.op("pool", (lambda e, t1=t1, t2=t2, j=j, c0=c0, n=n: e.tensor_tensor(mT[:, j, c0:c0 + n], t1, t2, ALU.add)),
                               deps=[mt1, mt2])
                    t_free[sb0] = mad
                    t_free[sb0 + 1] = mad
            wsl_free[wb] = last_mm
            wpa_free[g % 2] = last_mm
        e_done = [S.last["act"], S.last["dve"], S.last["pool"], S.last["pe"]]

    if _ph("F"):
        for e_ in ("pe", "act", "dve", "pool", "sp"):
            S.wait(e_, e_done)
        m_gp = S.dma("sp", "gpost", gpost[:, :], g_post.partition_broadcast(128), deps=e_done)
        m_gps = S.op("dve", lambda e: e.tensor_scalar(gpost[:, :], gpost[:, :], float(math.sqrt(D)), None, ALU.mult), deps=[m_gp])
        xh_free = [None, None]
        yh_free = [None, None]
        hn = [0]
        xld = {}

        def load_xh(k):
            t_, hf_ = k // 2, k % 2
            rows_ = 128 if t_ < 8 else 64
            hb_ = k % 2
            xsrc = xo[t_ * 128:(t_ + 1) * 128, hf_ * 1024:(hf_ + 1) * 1024] if t_ < 8 else xhs[16:80, hf_ * 1024:(hf_ + 1) * 1024]
            xld[k] = S.dma("sp", f"xh{hb_}", xh[hb_][0:rows_, :], xsrc, deps=[xh_free[hb_]] + e_done)

        load_xh(0)
        load_xh(1)

        pre = {0: pre_f[0]} if 0 in pre_f else {0: load_wo(0)}
        last_mm = {}
        fj_prev = [None]

        def f_group(cg, t):
            wb, m_w = pre[cg]
            w = wsl[wb]
            rows = 128 if t < 8 else 64
            c0 = t * 128 if t < 8 else CS0
            bnk = next_bank()
            outp = bk(bnk)[0:rows, :]
            mm = None
            for dc in range(16):
                mm = S.op("pe", (lambda e, outp=outp, lhsT=mT[:, dc, c0:c0 + rows], rhs=w[:, dc, :], dc=dc:
                                 e.matmul(outp, lhsT, rhs, start=(dc == 0), stop=(dc == 15))),
                          deps=[m_w, bank_free[bnk]] if dc == 0 else [], inc=(dc == 15))
            last_mm[cg] = mm
            ydst = y_acc[0:rows, t, cg * 512:(cg + 1) * 512]
            mev = evac(ydst, outp, [mm])
            bank_free[bnk] = mev
            msq = S.op("act", (lambda e: e.activation(
                fjunk[0:rows, :], ydst, AF.Square, accum_out=ssF[0:rows, t, cg:cg + 1])), deps=[mev, fj_prev[0]])
            fj_prev[0] = msq
            return mev, msq

        def f_epilogue(t, mev, msq):
            rows = 128 if t < 8 else 64
            m_s1 = S.op("dve", (lambda e: e.tensor_reduce(ssF1[0:rows, t:t + 1], ssF[0:rows, t, :],
                                                          mybir.AxisListType.X, ALU.add)), deps=[msq])
            m_rsa = S.op("act", (lambda e: e.activation(rsF[0:rows, t:t + 1], ssF1[0:rows, t:t + 1],
                                                        AF.Ln, bias=epsD[0:rows, 0:1])), deps=[m_s1, m_epsD])
            m_rs = S.op("act", (lambda e: e.activation(rsF[0:rows, t:t + 1], rsF[0:rows, t:t + 1],
                                                       AF.Exp, scale=-0.5)), deps=[m_rsa])
            for hf in range(2):
                kk_ = 2 * t + hf
                hb = kk_ % 2
                m_x = xld[kk_]
                Y = yh[hb]
                m_y = S.op("dve", (lambda e, Y=Y, hf=hf: e.scalar_tensor_tensor(
                    Y[0:rows, :], y_acc[0:rows, t, hf * 1024:(hf + 1) * 1024], rsF[0:rows, t:t + 1],
                    gpost[0:rows, hf * 1024:(hf + 1) * 1024], ALU.mult, ALU.mult)),
                    deps=[m_rs, m_gps, yh_free[hb], mev])
                m_add = S.op("pool", (lambda e, Y=Y, hb=hb: e.tensor_tensor(
                    Y[0:rows, :], Y[0:rows, :], xh[hb][0:rows, :], ALU.add)), deps=[m_y, m_x])
                xh_free[hb] = m_add
                dsty = yo[t * 128:(t + 1) * 128, hf * 1024:(hf + 1) * 1024] if t < 8 else ys[:, hf * 1024:(hf + 1) * 1024]
                m_st = S.dma("sp", f"yh{hb}", dsty, Y[0:rows, :], deps=[m_add])
                yh_free[hb] = m_st
                out_ms.append(m_st)
                if kk_ + 2 < 18:
                    load_xh(kk_ + 2)

        for cg in range(2):
            pre[cg + 1] = load_wo(cg + 1)
            for t in range(9):
                f_group(cg, t)
            wsl_free[pre[cg][0]] = last_mm[cg]
        pre[3] = load_wo(3)
        LAG = 1
        for k_ in range(9 + LAG):
            if k_ < 9:
                f_group(2, k_)
            if k_ - LAG >= 0:
                mev, msq = f_group(3, k_ - LAG)
                f_epilogue(k_ - LAG, mev, msq)
        wsl_free[pre[2][0]] = last_mm[2]
        wsl_free[pre[3][0]] = last_mm[3]

    S.wait("sp", out_ms)
    S.wait("sp", S.all_last())

    sem_names = set(S.ENG)
    for e_ in S.ENG:
        for it in S.q[e_]:
            if it[0] == "wait":
                sem_names.add(it[1])
            elif it[0] == "dma":
                sem_names.add(it[3])
    sem_names = sorted(sem_names)
    sems = {n: nc.alloc_semaphore(f"s_{n}") for n in sem_names}

    def run_queue(eng_obj, items, own):
        for it in items:
            if it[0] == "wait":
                eng_obj.wait_ge(sems[it[1]], it[2])
            elif it[0] == "op":
                ins = it[1](eng_obj)
                if it[2]:
                    ins.then_inc(sems[own], 1)
            else:
                _, out_, in_, key, kw = it
                eng_obj.dma_start(out=out_, in_=in_, **kw).then_inc(sems[key], 16)

    with nc.Block() as block:
        @block.tensor
        def _(e):
            run_queue(e, S.q["pe"], "pe")

        @block.scalar
        def _(e):
            run_queue(e, S.q["act"], "act")

        @block.vector
        def _(e):
            run_queue(e, S.q["dve"], "dve")

        @block.gpsimd
        def _(e):
            run_queue(e, S.q["pool"], "pool")

        @block.sync
        def _(e):
            run_queue(e, S.q["sp"], "sp")

    return nc


_PROGRAM = {}


def _get_program():
    key = (LAST_PHASE, DEBUG_DUMPS)
    if key not in _PROGRAM:
        _PROGRAM[key] = build_program()
    return _PROGRAM[key]


def kernel(x_prompt, x_sample, cache_k, cache_v, state_conv, norm_pre, norm_post,
           w_in, lambda_q1, lambda_k1, lambda_q2, lambda_k2, head_norm, conv_w,
           w_proj_attn, w_proj_conv, w_out, rel_bias):
    f32 = np.float32
    x_prompt = np.asarray(x_prompt, f32)
    x_sample = np.asarray(x_sample, f32)
    cache_k = np.asarray(cache_k, f32)
    cache_v = np.asarray(cache_v, f32)
    state_conv = np.asarray(state_conv, f32)
    w_in = np.asarray(w_in, f32)

    perm = _w_in_perm()
    w_l = np.ascontiguousarray(
        w_in[0][:, perm].reshape(16, 128, 24, 512).transpose(2, 1, 0, 3))
    wpa_l = np.ascontiguousarray(
        np.asarray(w_proj_attn, f32)[0].reshape(8, 128, 8, 256).transpose(2, 1, 0, 3))
    wpc_l = np.ascontiguousarray(
        np.asarray(w_proj_conv, f32)[0].reshape(8, 128, 8, 256).transpose(2, 1, 0, 3))
    wo_l = np.ascontiguousarray(
        np.asarray(w_out, f32)[0].reshape(16, 128, 2048).transpose(1, 0, 2))
    lam_in = np.concatenate([np.asarray(a, f32).reshape(-1) for a in
                             (lambda_q1, lambda_k1, lambda_q2, lambda_k2)]).reshape(1, 256)
    ident = np.eye(128, dtype=f32)
    ohd = np.concatenate([_onehot_window(0)] * 2, axis=0)
    ohp = np.concatenate([_onehot_window(-128)] * 2, axis=0)
    kk = np.arange(128)
    maskd = ((kk[:, None] // 64) <= (kk[None, :] // 64)).astype(f32)
    k64 = np.arange(64)
    bmask = ((k64[:, None] // 16) == (k64[None, :] // 16)).astype(f32)

    shared = {
        "w_l": w_l, "wpa_l": wpa_l, "wpc_l": wpc_l, "wo_l": wo_l,
        "g_pre": np.asarray(norm_pre, f32).reshape(1, D),
        "g_post": np.asarray(norm_post, f32).reshape(1, D),
        "g_head": np.asarray(head_norm, f32).reshape(1, 128),
        "conv_w": np.ascontiguousarray(np.asarray(conv_w, f32)[0]),
        "lam_in": lam_in,
        "rel_b": np.ascontiguousarray(np.asarray(rel_bias, f32)),
        "c_ident": ident, "c_ohd": ohd, "c_ohp": ohp, "c_maskd": maskd, "c_bmask": bmask,
    }
    in_maps = []
    for c in range(NCORES):
        b, par = c // 2, c % 2
        xb = x_prompt[b].reshape(16, 128, D)
        own = np.ascontiguousarray(xb[par::2].reshape(NOWN, D))
        oth = np.ascontiguousarray(xb[1 - par::2].reshape(NOWN, D))
        xhs = np.zeros((80, D), f32)
        for i in range(8):
            start = (2 * i + par) * 128
            if start >= 2:
                xhs[2 * i:2 * i + 2] = x_prompt[b, start - 2:start]
        xhs[16:80] = x_sample[4 * c:4 * c + 4].reshape(64, D)
        m = dict(shared)
        m.update({
            "xo": own, "xt": oth, "xhs": xhs,
            "ck": np.ascontiguousarray(
                cache_k[0, 4 * c:4 * c + 4].reshape(4, 16, 128, 4, 256).transpose(0, 3, 2, 1, 4)),
            "cv": np.ascontiguousarray(
                cache_v[0, 4 * c:4 * c + 4].reshape(4, 16, 128, 4, 256).transpose(0, 3, 2, 1, 4)),
            "sc": np.ascontiguousarray(state_conv[0, 4 * c:4 * c + 4]),
            "c_par": np.full((128, 1), float(par), f32),
        })
        in_maps.append(m)

    nc = _get_program()
    res = run_bass_kernel_spmd(nc, in_maps, core_ids=list(range(NCORES)))
    R = res.results

    y_prompt = np.zeros((4, 2048, D), f32)
    y_sample = np.zeros((32, 16, D), f32)
    nk_p = np.zeros((1, 4, 2048, NH, 128), f32)
    nv_p = np.zeros((1, 4, 2048, NH, 128), f32)
    nc_p = np.zeros((1, 4, 2, 1024), f32)
    nk_s = np.zeros((1, 32, 16, NH, 128), f32)
    nv_s = np.zeros((1, 32, 16, NH, 128), f32)
    nc_s = np.zeros((1, 32, 2, 1024), f32)
    for c in range(NCORES):
        b, par = c // 2, c % 2
        r = R[c]
        y_prompt[b].reshape(16, 128, D)[par::2] = np.asarray(r["yo"]).reshape(8, 128, D)
        y_sample[4 * c:4 * c + 4] = np.asarray(r["ys"]).reshape(4, 16, D)
        nk_p[0, b].reshape(16, 128, NH, 128)[par::2] = np.asarray(r["ko"]).reshape(8, 128, NH, 128)
        nv_p[0, b].reshape(16, 128, NH, 128)[par::2] = np.asarray(r["vo"]).reshape(8, 128, NH, 128)
        nk_s[0, 4 * c:4 * c + 4] = np.asarray(r["kso"]).reshape(4, 16, NH, 128)
        nv_s[0, 4 * c:4 * c + 4] = np.asarray(r["vso"]).reshape(4, 16, NH, 128)
        cvo = np.asarray(r["cvo"])
        if par == 1:
            nc_p[0, b] = cvo[0]
        nc_s[0, 4 * c:4 * c + 4] = cvo[1:5]
    return (y_prompt, y_sample, nk_p, nv_p, nc_p, nk_s, nv_s, nc_s)
```

```python
import math
import numpy as np
import concourse.bass as bass
import concourse.mybir as mybir
from concourse.bass_utils import run_bass_kernel_spmd

F32 = mybir.dt.float32
BF16 = mybir.dt.bfloat16
AF = mybir.ActivationFunctionType
ALU = mybir.AluOpType

D = 2048
NH = 8
NCORES = 8
EPS = 1e-6
NOWN = 1024
NCOL = 1104
CH0 = 1024
CS0 = 1040
SBASE = 16512

LAST_PHASE = "F"
DEBUG_DUMPS = False

D_SLOTS = 8
D_HEADS = 8
D_SAMPLE = True
D_EPI = True
PH_ORDER = ["S", "A", "B", "C1", "D", "C2", "E", "F"]


def _ph(name):
    return PH_ORDER.index(name) <= PH_ORDER.index(LAST_PHASE)


class Sched:
    ENG = ("pe", "act", "dve", "pool", "sp")

    def __init__(self):
        self.q = {e: [] for e in self.ENG}
        self.cnt = {e: 0 for e in self.ENG}
        self.dcnt = {}
        self.waited = {e: {} for e in self.ENG}
        self.last = {e: None for e in self.ENG}

    def wait(self, eng, ms):
        if ms is None:
            return
        if isinstance(ms, list):
            for m in ms:
                self.wait(eng, m)
            return
        key, val = ms
        assert key != "never", "waiting on an unresolved deferred milestone"
        if self.waited[eng].get(key, 0) >= val:
            return
        self.waited[eng][key] = val
        self.q[eng].append(("wait", key, val))

    def op(self, eng, fn, deps=(), inc=True):
        for d in deps:
            self.wait(eng, d)
        ms = None
        if inc:
            self.cnt[eng] += 1
            ms = (eng, self.cnt[eng])
            self.last[eng] = ms
        self.q[eng].append(("op", fn, inc))
        return ms

    def dma(self, eng, semkey, out, in_, deps=(), **kw):
        for d in deps:
            self.wait(eng, d)
        self.dcnt[semkey] = self.dcnt.get(semkey, 0) + 16
        self.q[eng].append(("dma", out, in_, semkey, kw))
        return (semkey, self.dcnt[semkey])

    def all_last(self):
        return [m for m in self.last.values() if m is not None]


def _np_bucket(rel):
    half = 16
    max_exact = 8
    ret = (rel > 0).astype(np.int32) * half
    n = np.abs(rel)
    nf = np.maximum(n, 1).astype(np.float32)
    large = max_exact + (np.log(nf / np.float32(max_exact)) / np.float32(math.log(128 / max_exact))
                         * np.float32(half - max_exact)).astype(np.int32)
    large = np.minimum(large, half - 1)
    return ret + np.where(n < max_exact, n, large)


def _onehot_window(offset):
    rel = np.arange(255, dtype=np.int32) - 127 + offset
    b = _np_bucket(rel)
    oh = np.zeros((32, 255), np.float32)
    oh[b, np.arange(255)] = 1.0
    return oh


def _w_in_perm():
    cols = []
    cols += list(range(1024, 2048))
    cols += list(range(2048, 3072))
    cols += list(range(0, 1024))
    cols += list(range(3072, 4096))
    for c in range(8):
        for base in (4096, 5120, 6144, 7168):
            cols += list(range(base + 128 * c, base + 128 * c + 128))
    for g in range(8):
        cols += list(range(8192 + 256 * g, 8192 + 256 * g + 256))
        cols += list(range(10240 + 256 * g, 10240 + 256 * g + 256))
    return np.asarray(cols, np.int64)


def build_program():
    nc = bass.Bass("TRN2", target_bir_lowering=False)
    S = Sched()

    def din(name, shape, dt=F32):
        return nc.dram_tensor(name, list(shape), dt, kind="ExternalInput").ap()

    def dout(name, shape, dt=F32):
        return nc.dram_tensor(name, list(shape), dt, kind="ExternalOutput").ap()

    xo = din("xo", [NOWN, D])
    xt = din("xt", [NOWN, D])
    xhs = din("xhs", [80, D])
    ck = din("ck", [4, 4, 128, 16, 256])
    cv = din("cv", [4, 4, 128, 16, 256])
    sc = din("sc", [4, 2, 1024])
    w_l = din("w_l", [24, 128, 16, 512])
    wpa_l = din("wpa_l", [8, 128, 8, 256])
    wpc_l = din("wpc_l", [8, 128, 8, 256])
    wo_l = din("wo_l", [128, 16, 2048])
    g_pre = din("g_pre", [1, D])
    g_post = din("g_post", [1, D])
    g_head = din("g_head", [1, 128])
    conv_w = din("conv_w", [3, 1024])
    lam_in = din("lam_in", [1, 256])
    rel_b = din("rel_b", [32, 8])
    c_ident = din("c_ident", [128, 128])
    c_ohd = din("c_ohd", [64, 255])
    c_ohp = din("c_ohp", [64, 255])
    c_maskd = din("c_maskd", [128, 128])
    c_bmask = din("c_bmask", [64, 64])
    c_par = din("c_par", [128, 1])

    yo = dout("yo", [NOWN, D])
    ys = dout("ys", [64, D])
    ko = dout("ko", [NOWN, 1024])
    vo = dout("vo", [NOWN, 1024])
    kso = dout("kso", [64, 1024])
    vso = dout("vso", [64, 1024])
    cvo = dout("cvo", [5, 2, 1024])
    dbg = {}

    def sb(name, shape, dt, off):
        nbytes = int(np.prod(shape[1:])) * (4 if dt == F32 else 2)
        assert off % 32 == 0, (name, off)
        assert SBASE + off + nbytes <= 229376, (name, off, nbytes)
        return nc.alloc_sbuf_tensor_at(name, list(shape), dt, offset=SBASE + off), off + ((nbytes + 31) // 32) * 32

    o = 0
    identf, o = sb("identf", [128, 128], F32, o)
    identb, o = sb("identb", [128, 128], BF16, o)
    bias_t = {}
    for nm_ in ("D", "P", "X1", "X2"):
        hi_, o = sb(f"b{nm_}h", [128, NH, 128], BF16, o)
        lo_, o = sb(f"b{nm_}l", [128, NH, 128], BF16, o)
        bias_t[nm_] = (hi_, lo_)
    bnew_h, o = sb("bnewh", [64, NH, 64], BF16, o)
    bnew_l, o = sb("bnewl", [64, NH, 64], BF16, o)
    negpar, o = sb("negpar", [128, 1], F32, o)
    ompar, o = sb("ompar", [128, 1], F32, o)
    hnb, o = sb("hnb", [128, 128], F32, o)
    convw, o = sb("convw", [128, 8, 3], F32, o)
    maskd, o = sb("maskd", [128, 128], F32, o)
    bmask, o = sb("bmask", [64, 64], F32, o)
    par, o = sb("par", [128, 1], F32, o)
    rb, o = sb("rb", [64, 8], F32, o)
    rb15, o = sb("rb15", [64, 8], F32, o)
    rbs, o = sb("rbs", [64, 8], F32, o)
    rbhf, o = sb("rbhf", [64, 8], F32, o)
    rbh, o = sb("rbh", [64, 8], BF16, o)
    rbs2, o = sb("rbs2", [64, 8], BF16, o)
    ohd, o = sb("ohd", [64, 255], F32, o)
    ohp, o = sb("ohp", [64, 255], F32, o)
    ohdb, o = sb("ohdb", [64, 256], BF16, o)
    ohpb, o = sb("ohpb", [64, 256], BF16, o)
    scT, o = sb("scT", [128, 8, 4, 2], F32, o)
    ncv, o = sb("ncv", [128, 8, 5, 2], F32, o)
    lamv, o = sb("lamv", [128, 256], F32, o)
    lams, o = sb("lams", [128, 8], F32, o)
    ssA, o = sb("ssA", [128, 20], F32, o)
    rsA, o = sb("rsA", [128, 20], F32, o)
    ssF, o = sb("ssF", [128, 9, 4], F32, o)
    ssF1, o = sb("ssF1", [128, 9], F32, o)
    rsF, o = sb("rsF", [128, 9], F32, o)
    epsD, o = sb("epsD", [128, 1], F32, o)
    eps128, o = sb("eps128", [128, 1], F32, o)
    est, o = sb("est", [128, 8, 8], F32, o)
    assert o <= 26624, o
    OFF_XNT = 26624
    xnT, _ = sb("xnT", [128, 16, NCOL], BF16, OFF_XNT)
    OFF_XO = 61952
    xnT_oth, _ = sb("xnT_oth", [128, 16, 1024], BF16, OFF_XO)
    QT, _ = sb("QT", [128, NH, NCOL], BF16, OFF_XO)
    zas, _ = sb("zas", [128, 8, 1024], BF16, OFF_XO + 17664)
    zass, _ = sb("zass", [16, 4, 1024], BF16, OFF_XO + 17664 + 16384)
    OFF_S2 = 104192
    NXST = 3
    OFF_KT_ = OFF_S2 + 32768
    OFF_VA_ = OFF_KT_ + 34048
    xst = [sb(f"xst{i}", [128, D], F32, OFF_VA_ + 8192 * i)[0] for i in range(NXST)]
    xnb = [sb(f"xnb{i}", [128, D], BF16, OFF_VA_ + 24576 + 4096 * i)[0] for i in range(2)]
    gpre, _ = sb("gpre", [128, D], F32, OFF_KT_)
    bf_D, _ = sb("bf_D", [128, NH, 128], F32, OFF_KT_ + 8192)
    bf_P, _ = sb("bf_P", [128, NH, 128], F32, OFF_KT_ + 12288)
    bf_T, _ = sb("bf_T", [128, NH, 128], F32, OFF_KT_ + 16384)
    bf_U, _ = sb("bf_U", [128, NH, 128], F32, OFF_KT_ + 20480)
    negm, _ = sb("negm", [128, 128], F32, OFF_KT_ + 24576)
    negb, _ = sb("negb", [64, 64], F32, OFF_KT_ + 25088)
    wsl = [sb(f"wsl{i}", [128, 16, 512], BF16, OFF_S2 + 16384 * i)[0] for i in range(2)]
    OFF_KT = OFF_S2 + 32768
    KT, _ = sb("KT", [128, NH, 2128], BF16, OFF_KT)
    OFF_VA = OFF_KT + 34048
    Vaug, _ = sb("Vaug", [128, 16, NH, 129], BF16, OFF_VA)
    OFF_VN = OFF_VA + 33024
    vnew, _ = sb("vnew", [64, NH, 129], BF16, OFF_VN)
    OFF_KVO = OFF_VN + 2080
    kvst = [sb(f"kvst{i}", [128, 512], F32, OFF_KVO + 2048 * i)[0] for i in range(3)]
    assert OFF_KVO + 6144 <= 212864

    o = OFF_S2
    Ksh = []
    Vsh = []
    KTsh = []
    for i in range(1):
        t, o = sb(f"KTsh{i}", [128, 2048], BF16, o)
        KTsh.append(t)
    Pt = []
    for i in range(3):
        t, o = sb(f"Pt{i}", [128, 2, 4, 128], BF16, o)
        Pt.append(t)
    Ptn_all, o = sb("Ptn_all", [64, NH, 2, 64], BF16, o)
    K0s, o = sb("K0s", [128, 16, 256], BF16, o)
    V0s, o = sb("V0s", [128, 16, 2, 129], BF16, o)
    assert o <= OFF_S2 + 32768, o
    o = OFF_KVO
    ep_t = []
    ep_d = []
    ep_oz = []
    for i in range(2):
        t, o = sb(f"ep_t{i}", [128, 128], F32, o)
        ep_t.append(t)
    for i in range(2):
        t, o = sb(f"ep_d{i}", [128, 128], F32, o)
        ep_d.append(t)
    for i in range(2):
        t, o = sb(f"ep_oz{i}", [128, 128], BF16, o)
        ep_oz.append(t)
    ep_junk, o = sb("ep_junk", [128, 128], F32, o)
    Pts = []
    for i in range(2):
        t, o = sb(f"Pts{i}", [128, 2, 16, 16], BF16, o)
        Pts.append(t)
    Ptn, o = sb("Ptn", [64, 2, 64], BF16, o)
    assert o <= 212864, o

    Vf2 = [sb(f"Vf2_{i}", [128, 16, 256], F32, OFF_KT + 16384 * i)[0] for i in range(2)]
    Ksh2 = [sb(f"Ksh2_{i}", [128, 16, 256], BF16, OFF_VA + 8192 * i)[0] for i in range(2)]
    Vsh2 = [sb(f"Vsh2_{i}", [128, 16, 2, 129], BF16, OFF_VA + 16384 + 8256 * i)[0] for i in range(2)]
    assert 16384 + 2 * 8256 <= 33024
    rT, _ = sb("rT", [128, 8, NCOL], BF16, OFF_KT)
    o = OFF_VA
    ubuf = []
    gbuf = []
    for i in range(2):
        t, o = sb(f"ubuf{i}", [128, 1112], F32, o)
        ubuf.append(t)
    for i in range(2):
        t, o = sb(f"gbuf{i}", [128, NCOL], F32, o)
        gbuf.append(t)
    c2h = []
    c2z = []
    for i in range(2):
        t, o = sb(f"c2h{i}", [128, 512], F32, o)
        c2h.append(t)
    for i in range(2):
        t, o = sb(f"c2z{i}", [128, 512], F32, o)
        c2z.append(t)
    ybuf, o = sb("ybufc", [128, NCOL], F32, o)
    assert o <= OFF_KVO, o

    wpa = [sb(f"wpa{i}", [128, 8, 256], BF16, OFF_KT + 17664 + 4096 * i)[0] for i in range(2)]
    wpc = [sb(f"wpc{i}", [128, 8, 256], BF16, OFF_KT + 17664 + 8192 + 4096 * i)[0] for i in range(2)]
    assert OFF_KT + 17664 + 16384 <= OFF_VA
    mT, _ = sb("mT", [128, 16, NCOL], BF16, OFF_VA)
    assert OFF_VA + 35328 <= 212864
    o = OFF_XO + 17664
    e_sg = []
    e_t = []
    for i in range(4):
        t, o = sb(f"e_sg{i}", [128, 512], F32, o)
        e_sg.append(t)
    for i in range(4):
        t, o = sb(f"e_t{i}", [128, 512], F32, o)
        e_t.append(t)
    assert o <= OFF_S2

    y_acc, _ = sb("y_acc", [128, 9, D], F32, OFF_XNT)
    assert OFF_XNT + 73728 <= OFF_S2
    gpost, _ = sb("gpost", [128, D], F32, OFF_KT)
    xh = [sb(f"xh{i}", [128, 1024], F32, OFF_KT + 8192 + 4096 * i)[0] for i in range(2)]
    yh = [sb(f"yh{i}", [128, 1024], F32, OFF_KT + 16384 + 4096 * i)[0] for i in range(2)]
    fjunk, _ = sb("fjunk", [128, 512], F32, OFF_KT + 24576)

    banks = [nc.alloc_psum_tensor(f"bank{i}", [128, 512], F32) for i in range(8)]

    def bk(i):
        return banks[i][:, :]

    def bk16(i):
        return banks[i][:, :].bitcast(BF16)

    bank_free = [None] * 8

    ld = {}
    ld["ident"] = S.dma("sp", "c0", identf[:, :], c_ident)
    S.dma("sp", "c1", rb[0:32, :], rel_b)
    ld["rb"] = S.dma("sp", "c1", rb[32:64, :], rel_b)
    ld["rb15"] = S.dma("sp", "c2", rb15[:, :], rel_b[15:16, :].partition_broadcast(64))
    ld["ohd"] = S.dma("sp", "c3", ohd[:, :], c_ohd)
    ld["ohp"] = S.dma("sp", "c4", ohp[:, :], c_ohp)
    ld["maskd"] = S.dma("sp", "c5", maskd[:, :], c_maskd)
    ld["bmask"] = S.dma("sp", "c6", bmask[:, :], c_bmask)
    ld["par"] = S.dma("sp", "c7", par[:, :], c_par)
    ld["gpre"] = S.dma("sp", "c8", gpre[:, :], g_pre.partition_broadcast(128))
    ld["hn"] = S.dma("sp", "c9", hnb[:, :], g_head.partition_broadcast(128))
    ld["lam"] = S.dma("sp", "c10", lamv[:, :], lam_in.partition_broadcast(128))

    m_identb = S.op("dve", lambda e: e.tensor_copy(identb[:, :], identf[:, :]), deps=[ld["ident"]])
    m_epsD = S.op("dve", lambda e: e.memset(epsD[:, :], float(D * EPS)))
    m_eps128 = S.op("dve", lambda e: e.memset(eps128[:, :], float(128.0 * EPS)))
    m_rbs0 = S.op("dve", lambda e: e.tensor_tensor(rbs[:, :], rb[:, :], rb15[:, :], ALU.subtract),
                  deps=[ld["rb"], ld["rb15"]])
    m_rbs0 = S.op("dve", lambda e: e.tensor_scalar(rbs[:, :], rbs[:, :], 8.0, None, ALU.mult), deps=[m_rbs0])
    m_r1 = S.op("dve", lambda e: e.tensor_copy(rbh[:, :], rbs[:, :]), deps=[m_rbs0])
    m_r2 = S.op("dve", lambda e: e.tensor_copy(rbhf[:, :], rbh[:, :]), deps=[m_r1])
    m_r3 = S.op("dve", lambda e: e.tensor_tensor(rbhf[:, :], rbs[:, :], rbhf[:, :], ALU.subtract), deps=[m_r2])
    m_r4 = S.op("dve", lambda e: e.tensor_copy(rbs2[0:32, :], rbh[0:32, :]), deps=[m_r1])
    m_r5 = S.op("dve", lambda e: e.tensor_copy(rbs2[32:64, :], rbhf[32:64, :]), deps=[m_r3])
    m_o1 = S.op("dve", lambda e: e.tensor_copy(ohdb[:, 0:255], ohd[:, :]), deps=[ld["ohd"]])
    m_o2 = S.op("dve", lambda e: e.tensor_copy(ohpb[:, 0:255], ohp[:, :]), deps=[ld["ohp"]])
    m_rbs = [m_r4, m_r5, m_o1, m_o2]
    m_gpre = S.op("dve", lambda e: e.tensor_scalar(gpre[:, :], gpre[:, :], float(math.sqrt(D)), None, ALU.mult),
                  deps=[ld["gpre"]])
    m_hnb = S.op("dve", lambda e: e.tensor_scalar(hnb[:, :], hnb[:, :], float(0.8 * math.sqrt(128.0)), None, ALU.mult),
                 deps=[ld["hn"]])
    m = S.op("dve", lambda e: e.scalar_tensor_tensor(lamv[:, 0:64], lamv[:, 0:64], 1.0, lamv[:, 64:128],
                                                      ALU.mult, ALU.mult, accum_out=lams[:, 0:1]),
             deps=[ld["lam"]])
    m2 = S.op("dve", lambda e: e.scalar_tensor_tensor(lamv[:, 128:192], lamv[:, 128:192], 1.0, lamv[:, 192:256],
                                                       ALU.mult, ALU.mult, accum_out=lams[:, 1:2]),
              deps=[ld["lam"]])
    m3 = S.op("act", lambda e: e.activation(lams[:, 2:4], lams[:, 0:2], AF.Exp), deps=[m, m2])
    m4 = S.op("dve", lambda e: e.tensor_tensor(lams[:, 4:5], lams[:, 2:3], lams[:, 3:4], ALU.subtract), deps=[m3])
    m_lam = S.op("dve", lambda e: e.tensor_scalar(lams[:, 5:6], lams[:, 4:5], 0.2, -1.0, ALU.add, ALU.mult), deps=[m4])

    m_ones2 = S.op("pool", lambda e: e.memset(vnew[:, :, 128:129], 1.0))

    a_done = []
    if _ph("A"):
        tiles = [("own", i) for i in range(8)] + [("hs", 0)] + [("oth", i) for i in range(8)]
        xst_free = [None] * NXST
        xnb_free = [None, None]
        tr_bank = [4, 5]
        tinfo = {}

        def stage_a1(ti):
            kind, i = tiles[ti]
            b = ti % 2
            bx = ti % NXST
            rows = 80 if kind == "hs" else 128
            src = {"own": xo, "oth": xt, "hs": xhs}[kind]
            srows = src[i * 128:(i + 1) * 128, :] if kind != "hs" else src[:, :]
            m_ld = S.dma("sp", f"xst{bx}", xst[bx][0:rows, :], srows, deps=[xst_free[bx]])
            xs_ = xst[bx]
            xb_ = xnb[b]
            m_sq = S.op("act", (lambda e: e.activation(
                xb_[0:rows, :], xs_[0:rows, :], AF.Square, accum_out=ssA[0:rows, ti:ti + 1])),
                deps=[m_ld, xnb_free[b]])
            m_ln = S.op("act", (lambda e: e.activation(
                rsA[0:rows, ti:ti + 1], ssA[0:rows, ti:ti + 1], AF.Ln, bias=epsD[0:rows, 0:1])),
                deps=[m_sq, m_epsD])
            m_rs = S.op("act", (lambda e: e.activation(
                rsA[0:rows, ti:ti + 1], rsA[0:rows, ti:ti + 1], AF.Exp, scale=-0.5)),
                deps=[m_ln])
            m_xn = S.op("dve", (lambda e: e.scalar_tensor_tensor(
                xb_[0:rows, :], xs_[0:rows, :], rsA[0:rows, ti:ti + 1], gpre[0:rows, :], ALU.mult, ALU.mult)),
                deps=[m_rs, m_gpre])
            xst_free[bx] = m_xn
            tinfo[ti] = (m_xn, xb_, rows, b)

        def stage_a2(ti):
            kind, i = tiles[ti]
            m_xn, xb_, rows, b = tinfo[ti]
            lastT = None
            for half in range(2):
                bnk = tr_bank[half]
                pv = bk16(bnk).rearrange("p (c t) -> p c t", t=128)
                mm = None
                for cc in range(8):
                    dc = half * 8 + cc
                    outp = pv[:, cc, 0:rows]
                    inp = xb_[0:rows, dc * 128:(dc + 1) * 128]
                    idn = identb[0:rows, 0:rows]
                    mm = S.op("pe", (lambda e, outp=outp, inp=inp, idn=idn: e.transpose(outp, inp, idn)),
                              deps=[m_xn, m_identb, bank_free[bnk]] if cc == 0 else [], inc=(cc == 7))
                lastT = mm
                if kind == "oth":
                    dst = xnT_oth[:, half * 8:(half + 1) * 8, i * 128:(i + 1) * 128]
                elif kind == "own":
                    dst = xnT[:, half * 8:(half + 1) * 8, i * 128:(i + 1) * 128]
                else:
                    dst = xnT[:, half * 8:(half + 1) * 8, CH0:CH0 + 80]
                srcp = pv[:, :, 0:rows]
                if half == 0:
                    mev = S.op("act", (lambda e, dst=dst, srcp=srcp: e.copy(dst, srcp)), deps=[mm])
                else:
                    mev = S.op("dve", (lambda e, dst=dst, srcp=srcp: e.tensor_copy(dst, srcp)), deps=[mm])
                bank_free[bnk] = mev
                a_done.append(mev)
            xnb_free[b] = lastT

        stage_a1(0)
        for ti in range(len(tiles)):
            if ti + 1 < len(tiles):
                stage_a1(ti + 1)
            stage_a2(ti)
        a_done = a_done[-6:] + [S.last["act"], S.last["dve"]]
    for k_ in range(3):
        ld["convw"] = S.dma("sp", "c11", convw[:, :, k_], conv_w[k_].rearrange("(c p) -> p c", p=128),
                            allow_slow_non_contiguous=True)
    for s_ in range(4):
        for j_ in range(2):
            ld["scT"] = S.dma("sp", "c12", scT[:, :, s_, j_], sc[s_, j_].rearrange("(c p) -> p c", p=128),
                              allow_slow_non_contiguous=True)
    m_ones = S.op("pool", lambda e: e.memset(Vaug[:, :, :, 128:129], 1.0), deps=a_done)
    BIG = 30000.0

    def gen_bias(oh, dstf, bank_a, bank_b):
        pa = bk(bank_a).rearrange("p (q h) -> p q h", h=8)
        pb = bk(bank_b).rearrange("p (q h) -> p q h", h=8)
        last = None
        for q in range(128):
            pbk = pa if q < 64 else pb
            qq = q % 64
            lhsT = oh[:, 127 - q:255 - q]
            outp = pbk[:, qq, :]
            islast = (q == 63 or q == 127)
            mm = S.op("pe", (lambda e, outp=outp, lhsT=lhsT: e.matmul(outp, lhsT, rbs2[:, :], start=True, stop=True)),
                      deps=[m_rbs, bank_free[bank_a], bank_free[bank_b]] + a_done if q == 0 else [],
                      inc=islast)
            if islast:
                last = mm
        ms = []
        for half, pbk in ((0, pa), (1, pb)):
            src = pbk.rearrange("p q h -> p h q")
            d = dstf[:, :, half * 64:(half + 1) * 64]
            ms.append(S.op("act", (lambda e, d=d, src=src: e.copy(d, src)), deps=[last] + a_done))
        bank_free[bank_a] = ms[0]
        bank_free[bank_b] = ms[1]
        return ms

    mD = gen_bias(ohdb, bf_D, 0, 1)
    mP = gen_bias(ohpb, bf_P, 2, 3)
    m_np = S.op("dve", lambda e: e.tensor_scalar(negpar[:, :], par[:, :], -1.0, BIG, ALU.add, ALU.mult), deps=[ld["par"]])
    m_op = S.op("dve", lambda e: e.tensor_scalar(ompar[:, :], par[:, :], -1.0, 1.0, ALU.mult, ALU.add), deps=[ld["par"]])
    m_ng = S.op("dve", lambda e: e.tensor_scalar(negm[:, :], maskd[:, :], -1.0, BIG, ALU.add, ALU.mult), deps=[ld["maskd"]] + a_done)
    m_nb = S.op("dve", lambda e: e.tensor_scalar(negb[:, :], bmask[:, :], -1.0, BIG, ALU.add, ALU.mult), deps=[ld["bmask"]] + a_done)

    def split_hilo(srcf, hi_, lo_, tmpf_, deps):
        h1 = S.op("dve", lambda e: e.tensor_copy(hi_, srcf), deps=deps)
        h2 = S.op("dve", lambda e: e.tensor_copy(tmpf_, hi_), deps=[h1])
        h3 = S.op("dve", lambda e: e.tensor_tensor(tmpf_, srcf, tmpf_, ALU.subtract), deps=[h2])
        h4 = S.op("dve", lambda e: e.tensor_copy(lo_, tmpf_), deps=[h3])
        return [h1, h4]

    n1 = S.op("dve", lambda e: e.tensor_tensor(bf_T[0:64, :, 0:64], bf_D[0:64, :, 0:64],
                                               bmask[:, :].unsqueeze(1).to_broadcast([64, NH, 64]), ALU.mult), deps=mD + [ld["bmask"]])
    n2 = S.op("dve", lambda e: e.tensor_tensor(bf_T[0:64, :, 0:64], bf_T[0:64, :, 0:64],
                                               negb[:, :].unsqueeze(1).to_broadcast([64, NH, 64]), ALU.add), deps=[n1, m_nb])
    m_bnew = split_hilo(bf_T[0:64, :, 0:64], bnew_h[:, :, :], bnew_l[:, :, :], bf_U[0:64, :, 0:64], [n2])
    m_bP = split_hilo(bf_P[:, :, :], bias_t["P"][0][:, :, :], bias_t["P"][1][:, :, :], bf_U[:, :, :], mP + m_bnew)
    x1 = S.op("dve", lambda e: e.tensor_scalar(bf_T[:, :, :], bf_P[:, :, :], par[:, 0:1], negpar[:, 0:1], ALU.mult, ALU.add),
              deps=mP + m_bnew + [m_np])
    m_bX1 = split_hilo(bf_T[:, :, :], bias_t["X1"][0][:, :, :], bias_t["X1"][1][:, :, :], bf_U[:, :, :], [x1] + m_bP)
    x2 = S.op("dve", lambda e: e.tensor_scalar(bf_T[:, :, :], bf_P[:, :, :], ompar[:, 0:1], None, ALU.mult),
              deps=m_bX1 + [m_op])
    m_bX2 = split_hilo(bf_T[:, :, :], bias_t["X2"][0][:, :, :], bias_t["X2"][1][:, :, :], bf_U[:, :, :], [x2])
    d1 = S.op("dve", lambda e: e.tensor_tensor(bf_T[:, :, :], bf_D[:, :, :],
                                               maskd[:, :].unsqueeze(1).to_broadcast([128, NH, 128]), ALU.mult), deps=m_bX2 + mD)
    d2 = S.op("dve", lambda e: e.tensor_tensor(bf_T[:, :, :], bf_T[:, :, :],
                                               negm[:, :].unsqueeze(1).to_broadcast([128, NH, 128]), ALU.add), deps=[d1, m_ng])
    m_bD = split_hilo(bf_T[:, :, :], bias_t["D"][0][:, :, :], bias_t["D"][1][:, :, :], bf_U[:, :, :], [d2])
    m_bias = m_bD + m_bX2 + m_bX1 + m_bP + m_bnew
    setup_ms = [m_identb, m_gpre, m_hnb, m_lam, m_ones, m_ones2, ld["convw"], ld["scT"]] + m_bias

    wsl_free = [None, None]
    wsl_n = [0]

    def load_slab(idx, extra_deps=()):
        b = wsl_n[0] % 2
        wsl_n[0] += 1
        m_ = S.dma("pool", f"wsl{b}", wsl[b][:, :, :], w_l[idx], deps=[wsl_free[b]] + list(extra_deps))
        return b, m_

    ev_rr = [0]

    def evac(dst, src, deps, func=None, scale=None):
        ev_rr[0] += 1
        if func is not None or ev_rr[0] % 2 == 0:
            f = func if func is not None else AF.Copy
            if scale is None:
                return S.op("act", (lambda e: e.activation(dst, src, f)), deps=deps)
            return S.op("act", (lambda e: e.activation(dst, src, f, scale=scale)), deps=deps)
        return S.op("dve", (lambda e: e.tensor_copy(dst, src)), deps=deps)

    pb_rr = [0]

    def next_bank():
        b = pb_rr[0] % 8
        pb_rr[0] += 1
        return b

    out_ms = []

    if _ph("B"):
        kvst_free = [None] * 3
        kv_n = [0]
        kf_n = [0]
        a_dep = a_done
        kt_dep = m_bias
        pend_tr = []

        def flush_tr():
            for (kfs, mf, fa, n, c0, h) in pend_tr:
                tb_ = next_bank()
                tps = bk(tb_)
                if n == 512:
                    blocks = [(k_ * 128, 128, k_) for k_ in range(4)]
                else:
                    blocks = [(16, 64, 0)]
                mt_ = None
                for bi, (o0, nt_, k_) in enumerate(blocks):
                    mt_ = S.op("pe", (lambda e, o0=o0, nt_=nt_, k_=k_, tps=tps, kfs=kfs: e.transpose(
                        tps[0:nt_, k_ * 128:(k_ + 1) * 128], kfs[:, o0:o0 + nt_], identf[:, :])),
                        deps=[mf, bank_free[tb_]] if bi == 0 else [], inc=(bi == len(blocks) - 1))
                kvst_free[fa] = mt_
                nb_ = len(blocks)
                rows_ = blocks[0][1]
                kts = kvst[2]
                me3 = S.op("dve" if fa == 0 else "act",
                           (lambda e, kts=kts, tps=tps, nb_=nb_, rows_=rows_, fa=fa: (
                               e.tensor_copy(kts[0:rows_, 0:nb_ * 128], tps[0:rows_, 0:nb_ * 128]) if fa == 0
                               else e.copy(kts[0:rows_, 0:nb_ * 128], tps[0:rows_, 0:nb_ * 128]))),
                           deps=[mt_, kvst_free[2]])
                bank_free[tb_] = me3
                if n == 512:
                    t0 = c0 // 128
                    dstd = ko.rearrange("(tb p) c -> p tb c", p=128)[:, t0:t0 + 4, h * 128:(h + 1) * 128]
                    srcd = kts[:, :].rearrange("p (tb c) -> p tb c", c=128)
                else:
                    dstd = kso[:, h * 128:(h + 1) * 128]
                    srcd = kts[0:64, 0:128]
                md = S.dma("sp", "kvst2", dstd, srcd, deps=[me3])
                kvst_free[2] = md
                out_ms.append(md)

            pend_tr.clear()

        b_order = [2, 3, 0, 1]
        pre = {b_order[0]: load_slab(b_order[0])}
        for oi, sl in enumerate(b_order):
            if oi + 1 < 4:
                pre[b_order[oi + 1]] = load_slab(b_order[oi + 1])
            wb, m_w = pre[sl]
            w = wsl[wb]
            last_use = []
            if sl < 2:
                groups = [(xnT, 0, 512, 0), (xnT, 512, 512, 512), (xnT, 1024, 80, 2048),
                          (xnT_oth, 0, 512, 1024), (xnT_oth, 512, 512, 1536)]
                for hh in range(4):
                    h = 4 * sl + hh
                    for (X, c0, n, dst0) in groups:
                        bnk = next_bank()
                        outp = bk(bnk)[:, 0:n]
                        for dc in range(16):
                            lhsT = w[:, dc, hh * 128:(hh + 1) * 128]
                            rhs = X[:, dc, c0:c0 + n]
                            mm = S.op("pe", (lambda e, outp=outp, lhsT=lhsT, rhs=rhs, dc=dc: e.matmul(
                                outp, lhsT, rhs, start=(dc == 0), stop=(dc == 15))),
                                deps=[m_w, bank_free[bnk]] + a_dep if dc == 0 else [], inc=(dc == 15))
                        flush_tr()
                        mev = evac(KT[:, h, dst0:dst0 + n], outp, [mm] + kt_dep)
                        bank_free[bnk] = mev
                        last_use = [mm]
                        if X is xnT:
                            fa = kf_n[0] % 2
                            kf_n[0] += 1
                            kfs = kvst[fa]
                            mf = S.op("act" if fa == 0 else "dve",
                                      (lambda e, kfs=kfs, outp=outp, n=n, fa=fa: (e.copy(kfs[:, 0:n], outp) if fa == 0
                                                                              else e.tensor_copy(kfs[:, 0:n], outp))),
                                      deps=[mm, kvst_free[fa], mev])
                            bank_free[bnk] = [mev, mf]
                            pend_tr.append((kfs, mf, fa, n, c0, h))
            else:
                flush_tr()
                vs = sl - 2
                vt = [("own", i) for i in range(8)] + [("oth", i) for i in range(8)] + [("smp", 0)]
                for (kind, i) in vt:
                    rows = 64 if kind == "smp" else 128
                    if kind == "own":
                        X, c0 = xnT, i * 128
                    elif kind == "oth":
                        X, c0 = xnT_oth, i * 128
                    else:
                        X, c0 = xnT, CS0
                    bnk = next_bank()
                    outp = bk(bnk)[0:rows, :]
                    for dc in range(16):
                        lhsT = X[:, dc, c0:c0 + rows]
                        rhs = w[:, dc, :]
                        mm = S.op("pe", (lambda e, outp=outp, lhsT=lhsT, rhs=rhs, dc=dc: e.matmul(
                            outp, lhsT, rhs, start=(dc == 0), stop=(dc == 15))),
                            deps=[m_w, bank_free[bnk]] + a_dep if dc == 0 else [], inc=(dc == 15))
                    last_use = [mm]
                    if kind == "oth":
                        dstv = Vaug[:, 8 + i, 4 * vs:4 * vs + 4, 0:128]
                        mev = evac(dstv, outp.rearrange("p (h e) -> p h e", e=128), [mm])
                        bank_free[bnk] = mev
                    else:
                        sb_ = kv_n[0] % 3
                        kv_n[0] += 1
                        st = kvst[sb_]
                        mev = evac(st[0:rows, :], outp, [mm, kvst_free[sb_]])
                        bank_free[bnk] = mev
                        if kind == "own":
                            dstv = Vaug[:, i, 4 * vs:4 * vs + 4, 0:128]
                            dstd = vo[i * 128:(i + 1) * 128, vs * 512:(vs + 1) * 512]
                        else:
                            dstv = vnew[:, 4 * vs:4 * vs + 4, 0:128]
                            dstd = vso[:, vs * 512:(vs + 1) * 512]
                        srcv = st[0:rows, :].rearrange("p (h e) -> p h e", e=128)
                        mp = S.op("pool", (lambda e, dstv=dstv, srcv=srcv: e.tensor_copy(dstv, srcv)), deps=[mev])
                        md = S.dma("sp", f"kvst{sb_}", dstd, st[0:rows, :], deps=[mev])
                        kvst_free[sb_] = [mp, md]
                        out_ms.append(md)
            wsl_free[wb] = last_use
        b_done = [S.last["act"], S.last["dve"], S.last["pool"], S.last["pe"]]

    if _ph("C1"):
        c1_dep = [S.last["pe"]] + b_done
        pre = {}
        pre[4] = load_slab(4)
        for sl in range(4, 8):
            if sl + 1 < 8:
                pre[sl + 1] = load_slab(sl + 1)
            wb, m_w = pre[sl]
            w = wsl[wb]
            last_use = []
            if sl < 6:
                for hh in range(4):
                    h = 4 * (sl - 4) + hh
                    for (c0, n) in ((0, 512), (512, 512), (1024, 80)):
                        bnk = next_bank()
                        outp = bk(bnk)[:, 0:n]
                        for dc in range(16):
                            lhsT = w[:, dc, hh * 128:(hh + 1) * 128]
                            rhs = xnT[:, dc, c0:c0 + n]
                            mm = S.op("pe", (lambda e, outp=outp, lhsT=lhsT, rhs=rhs, dc=dc: e.matmul(
                                outp, lhsT, rhs, start=(dc == 0), stop=(dc == 15))),
                                deps=[m_w, bank_free[bnk]] if dc == 0 else [], inc=(dc == 15))
                        mev = evac(QT[:, h, c0:c0 + n], outp, [mm] + c1_dep)
                        bank_free[bnk] = mev
                        last_use = [mm]
            else:
                zs = sl - 6
                jobs = [("own", i) for i in range(8)] + [("smp", s) for s in range(4)]
                for (kind, i) in jobs:
                    rows = 128 if kind == "own" else 16
                    c0 = i * 128 if kind == "own" else CS0 + 16 * i
                    bnk = next_bank()
                    outp = bk(bnk)[0:rows, :]
                    for dc in range(16):
                        lhsT = xnT[:, dc, c0:c0 + rows]
                        rhs = w[:, dc, :]
                        mm = S.op("pe", (lambda e, outp=outp, lhsT=lhsT, rhs=rhs, dc=dc: e.matmul(
                            outp, lhsT, rhs, start=(dc == 0), stop=(dc == 15))),
                            deps=[m_w, bank_free[bnk]] if dc == 0 else [], inc=(dc == 15))
                    last_use = [mm]
                    sb_ = kv_n[0] % 3
                    kv_n[0] += 1
                    st = kvst[sb_]
                    stv = st[0:rows, :]
                    mev = S.op("act", (lambda e, stv=stv, outp=outp: e.activation(stv, outp, AF.Silu)),
                               deps=[mm, kvst_free[sb_]])
                    bank_free[bnk] = mev
                    if kind == "own":
                        dstz = zas[:, i, zs * 512:(zs + 1) * 512].rearrange("p (h e) -> p h e", e=128)
                    else:
                        dstz = zass[:, i, zs * 512:(zs + 1) * 512].rearrange("p (h e) -> p h e", e=128)
                    srcz = stv.rearrange("p (h e) -> p h e", e=128)
                    hb = hnb[0:rows, :].unsqueeze(1).to_broadcast([rows, 4, 128])
                    mp = S.op("pool", (lambda e, dstz=dstz, srcz=srcz, hb=hb: e.tensor_tensor(dstz, srcz, hb, ALU.mult)),
                              deps=[mev, m_hnb] + c1_dep)
                    kvst_free[sb_] = mp
            wsl_free[wb] = last_use
        c1_done = [S.last["act"], S.last["dve"], S.last["pool"], S.last["pe"]]

    if _ph("D"):
        d_dep = c1_done + setup_ms + [m for m in out_ms[-3:]]
        d_dep = d_dep + [x for x in (wsl_free[0], wsl_free[1]) if x is not None]
        for x in kvst_free:
            if x is not None:
                d_dep.append(x)
        S.wait("pe", d_dep)
        S.wait("act", d_dep)
        S.wait("dve", d_dep)
        S.wait("pool", d_dep)
        S_SETS = [(0, 1), (2, 3)]
        m_v0one = S.op("pool", lambda e: e.memset(V0s[:, :, :, 128:129], 1.0))
        m_k0s = S.dma("pool", "k0s", K0s[:, :, :], ck[0, 0])
        m_v0s = [S.dma("pool", f"v0s{hh_}", V0s[:, :, hh_, 0:128], cv[0, 0][:, :, hh_ * 128:(hh_ + 1) * 128], deps=[m_v0one])
                 for hh_ in range(2)]
        s_n = [0]
        pt_free = [None] * 3
        ep_free = [None, None]
        ep_n = [0]
        est_n = [0]

        pending = []
        epj_prev = [None]
        pendingB = []

        def flush_B(n, force=False):
            keep = []
            for (due, fn) in pendingB:
                if force or due <= n:
                    fn()
                else:
                    keep.append((due, fn))
            pendingB[:] = keep

        def flush_pending(n, force=False):
            keep = []
            for (due, fn) in pending:
                if force or due <= n:
                    fn()
                else:
                    keep.append((due, fn))
            pending[:] = keep

        def epilogue(rows, O1, O2, zsrc, dst_cols, h, par_b, o_banks, o_ready, n_step):
            k = est_n[0] % 8
            est_n[0] += 1
            st = est[0:rows, k, :]
            pb_ = par_b
            t_ = ep_t[pb_][0:rows, :]
            d_ = ep_d[pb_][0:rows, :]
            oz_ = ep_oz[pb_][0:rows, :]
            deps0 = [o_ready]
            a1 = S.op("dve", (lambda e: e.reciprocal(st[:, 0:1], O1[:, 128:129])), deps=deps0)
            a2 = S.op("dve", (lambda e: e.reciprocal(st[:, 1:2], O2[:, 128:129])), deps=deps0)
            a3 = S.op("dve", (lambda e: e.tensor_tensor(st[:, 2:3], st[:, 1:2], lams[0:rows, 5:6], ALU.mult)),
                      deps=[a2, m_lam])
            a4 = S.op("dve", (lambda e: e.tensor_scalar(t_, O2[:, 0:128], st[:, 2:3], None, ALU.mult)),
                      deps=[a3, ep_free[pb_]])
            a5 = S.op("dve", (lambda e: e.scalar_tensor_tensor(d_, O1[:, 0:128], st[:, 0:1], t_, ALU.mult, ALU.add)),
                      deps=[a1, a4])
            bank_free[o_banks[1]] = a5
            bank_free[o_banks[0]] = ("never", 1 << 30)
            ep_free[pb_] = ("never", 1 << 30)
            hold = {}

            def partB():
                a6 = S.op("act", (lambda e: e.activation(ep_junk[0:rows, :], d_, AF.Square, accum_out=st[:, 3:4])),
                          deps=[a5, epj_prev[0]])
                epj_prev[0] = a6
                a7a = S.op("act", (lambda e: e.activation(st[:, 4:5], st[:, 3:4], AF.Ln, bias=eps128[0:rows, 0:1])),
                           deps=[a6, m_eps128])
                a7 = S.op("act", (lambda e: e.activation(st[:, 4:5], st[:, 4:5], AF.Exp, scale=-0.5)), deps=[a7a])
                hold["a8"] = S.op("dve", (lambda e: e.scalar_tensor_tensor(oz_, d_, st[:, 4:5], zsrc, ALU.mult, ALU.mult)),
                                  deps=[a7, a6])

            def partC():
                tp = bk16(o_banks[0])[:, 0:rows]
                a9 = S.op("pe", (lambda e: e.transpose(tp, oz_, identb[0:rows, 0:rows])), deps=[hold["a8"], a5])
                a10 = S.op("act", (lambda e: e.copy(QT[:, h, dst_cols[0]:dst_cols[1]], tp)), deps=[a9])
                bank_free[o_banks[0]] = a10
                ep_free[pb_] = a9
            pendingB.append((n_step + 1, partB))
            pending.append((n_step + 2, partC))

        steps = []
        for i in range(D_SLOTS):
            far = [("own", j, None) for j in range(i)] + [("oth", j, None) for j in range(max(0, i - 1))]
            near = ([("oth", i - 1, "X2")] if i >= 1 else []) + [("oth", i, "X1"), ("own", i, "D")]
            tl = far + near
            n_t = len(tl)
            groups = []
            idx = n_t % 4
            if idx:
                groups.append(tl[0:idx])
            while idx < n_t:
                groups.append(tl[idx:idx + 4])
                idx += 4
            for h in range(D_HEADS):
                pe_par = ep_n[0] % 2
                ep_n[0] += 1
                ob = (4, 5) if pe_par == 0 else (6, 7)
                for gi, grp in enumerate(groups):
                    steps.append(dict(i=i, h=h, gi=gi, grp=grp, ng=len(groups), ob=ob, par=pe_par))

        def stage_S(st, n):
            sset = S_SETS[n % 2]
            grp = st["grp"]
            nt = len(grp)
            i, h = st["i"], st["h"]
            Sm = [bk(sset[m_])[:, 0:nt * 128].rearrange("p (t q) -> p t q", q=128) for m_ in range(2)]
            first = True
            mm = None
            for ti, (kind, j, mult) in enumerate(grp):
                kc = (j * 128) if kind == "own" else (1024 + j * 128)
                for m_ in range(2):
                    lhsT = KT[64 * m_:64 * m_ + 64, h, kc:kc + 128]
                    rhs = QT[64 * m_:64 * m_ + 64, h, i * 128:(i + 1) * 128]
                    outp = Sm[m_][:, ti, :]
                    mm = S.op("pe", (lambda e, outp=outp, lhsT=lhsT, rhs=rhs, near_=(mult is not None): e.matmul(
                        outp, lhsT, rhs, start=True, stop=(not near_))),
                        deps=[bank_free[sset[0]], bank_free[sset[1]]] if first else [],
                        inc=(ti == nt - 1 and m_ == 1 and mult is None))
                    first = False
                if mult is not None:
                    for m_ in range(2):
                        outp = Sm[m_][:, ti, :]
                        for part in range(2):
                            rhs = bias_t[mult][part][:, h, :]
                            mm = S.op("pe", (lambda e, outp=outp, rhs=rhs, part=part: e.matmul(
                                outp, identb[:, :], rhs, start=False, stop=(part == 1))),
                                deps=[], inc=(ti == nt - 1 and m_ == 1 and part == 1))
            st["Sm"] = Sm
            st["mm"] = mm
            st["sset"] = sset

        def stage_X(st, n):
            grp = st["grp"]
            nt = len(grp)
            h = st["h"]
            pbuf = n % 3
            P_ = Pt[pbuf]
            Sm, mm, sset = st["Sm"], st["mm"], st["sset"]
            mexs = []
            for m_ in range(2):
                mexs.append(S.op("act", (lambda e, P_=P_, src=Sm[m_], m_=m_, nt=nt: e.activation(
                    P_[:, m_, 0:nt, :], src, AF.Exp, scale=0.125)), deps=[mm, pt_free[pbuf]]))
                bank_free[sset[m_]] = mexs[-1]
            p_ready = mexs
            st["p_ready"] = p_ready
            st["P"] = P_
            st["pbuf"] = pbuf

        def stage_AV(st, n):
            grp = st["grp"]
            nt = len(grp)
            i, h, gi, ng, ob = st["i"], st["h"], st["gi"], st["ng"], st["ob"]
            P_ = st["P"]
            O1 = bk(ob[0])[:, 0:129]
            O2 = bk(ob[1])[:, 0:129]
            Os = (O1, O2)
            mm2 = None
            for ti, (kind, j, mult) in enumerate(grp):
                vt_ = j if kind == "own" else 8 + j
                for m_ in range(2):
                    lhsT = P_[:, m_, ti, :]
                    rhs = Vaug[:, vt_, h, :]
                    isfirst = (gi == 0 and ti == 0)
                    islast = (gi == ng - 1 and ti == nt - 1)
                    dps = list(st["p_ready"]) if (ti == 0 and m_ == 0) else []
                    if isfirst:
                        dps.append(bank_free[ob[m_]])
                    outp = Os[m_]
                    mm2 = S.op("pe", (lambda e, outp=outp, lhsT=lhsT, rhs=rhs, isfirst=isfirst, islast=islast:
                                      e.matmul(outp, lhsT, rhs, start=isfirst, stop=islast)),
                               deps=dps, inc=(ti == nt - 1 and m_ == 1))
            pt_free[st["pbuf"]] = mm2
            if gi == ng - 1 and D_EPI:
                zsrc = zas[:, i, h * 128:(h + 1) * 128]
                epilogue(128, O1, O2, zsrc, (i * 128, (i + 1) * 128), h, st["par"], ob, mm2, n)

        if steps:
            stage_S(steps[0], 0)
        for n, st in enumerate(steps):
            stage_X(st, n)
            flush_B(n)
            if n + 1 < len(steps):
                stage_S(steps[n + 1], n + 1)
            flush_pending(n)
            stage_AV(st, n)
        flush_B(0, force=True)
        flush_pending(0, force=True)

        pool_ones = []
        ksh_free = [None, None]
        vsh_free = [None, None]
        ktsh_free = [None]
        pts_free = [None, None]
        ptn_used = [None]
        jobs = [dict(h=2 * hp + hh, s=s_, hh=hh) for hp in range(4 if D_SAMPLE else 0) for s_ in range(4) for hh in range(2)]
        pinfo = {}
        vf2_free = [None, None]
        ksh2_free = [None, None]
        vsh2_free = [None, None]

        def pair_dma(p):
            hp, s_ = p // 4, p % 4
            b = p % 2
            if p == 0:
                pinfo[p] = [m_k0s, None, m_k0s, None]
                return
            mk = S.dma("pool", f"ksh2{b}", Ksh2[b][:, :, :], ck[s_, hp], deps=[ksh2_free[b]] + sample_start)
            mvf = S.dma("sp", f"vf2{b}", Vf2[b][:, :, :], cv[s_, hp], deps=[vf2_free[b]] + sample_start)
            pinfo[p] = [mk, mvf, mk, None]

        def pair_cast(p):
            b = p % 2
            pass
        head_ptn = {}

        def stage_N(h):
            nb0 = 0 if h % 2 == 0 else 2
            mm = None
            for m_ in range(2):
                lhsT = KT[64 * m_:64 * m_ + 64, h, 2064:2128]
                rhs = QT[64 * m_:64 * m_ + 64, h, CS0:CS0 + 64]
                outp = bk(nb0 + m_)[0:64, 0:64]
                mm = S.op("pe", (lambda e, outp=outp, lhsT=lhsT, rhs=rhs: e.matmul(outp, lhsT, rhs, start=True, stop=False)),
                          deps=[bank_free[nb0], bank_free[nb0 + 1]] if m_ == 0 else [], inc=False)
            for m_ in range(2):
                outp = bk(nb0 + m_)[0:64, 0:64]
                for part, bt in enumerate((bnew_h, bnew_l)):
                    rhs = bt[:, h, :]
                    mm = S.op("pe", (lambda e, outp=outp, rhs=rhs, part=part: e.matmul(
                        outp, identb[0:64, 0:64], rhs, start=False, stop=(part == 1))),
                        deps=[], inc=(m_ == 1 and part == 1))
            mexn = []
            for m_ in range(2):
                mexn.append(S.op("act", (lambda e, m_=m_, h=h, nb0=nb0: e.activation(Ptn_all[:, h, m_, :], bk(nb0 + m_)[0:64, 0:64], AF.Exp, scale=0.125)),
                                 deps=[mm]))
                bank_free[nb0 + m_] = mexn[-1]
            head_ptn[h] = mexn

        def stage_T(jb, n):
            h, s_ = jb["h"], jb["s"]
            p = n // 2
            hh = n % 2
            b = p % 2
            if hh == 0:
                pair_cast(p)
                if p + 1 < len(jobs) // 2:
                    pair_dma(p + 1)
            if p == 0:
                K_ = K0s[:, :, hh * 128:(hh + 1) * 128]
                V_ = V0s[:, :, hh, :]
            else:
                K_ = Ksh2[b][:, :, hh * 128:(hh + 1) * 128]
                V_ = Vsh2[b][:, :, hh, :]
            KT_ = KTsh[0]
            mk = pinfo[p][2]
            mvf = pinfo[p][1]
            mts = []
            mt = None
            for half in range(2):
                bnk = half
                pv = bk16(bnk).rearrange("p (c t) -> p c t", t=128)
                for cc in range(8):
                    kt = half * 8 + cc
                    outp = pv[:, cc, :]
                    inp = K_[:, kt, :]
                    mt = S.op("pe", (lambda e, outp=outp, inp=inp: e.transpose(outp, inp, identb[:, :])),
                              deps=[mk, bank_free[bnk]] if cc == 0 else [], inc=(cc == 7))
                dst = KT_[:, half * 1024:(half + 1) * 1024]
                srcp = bk16(bnk)
                if half == 0:
                    mev = S.op("act", (lambda e, dst=dst, srcp=srcp: e.copy(dst, srcp)), deps=[mt, ktsh_free[0]])
                else:
                    mev = S.op("dve", (lambda e, dst=dst, srcp=srcp: e.tensor_copy(dst, srcp)), deps=[mt, ktsh_free[0]])
                bank_free[bnk] = mev
                mts.append(mev)
            if hh == 1 and p != 0:
                ksh2_free[b] = mt
            if p == 0:
                mv = m_v0s
            else:
                mv = S.op("dve", (lambda e: e.tensor_copy(Vsh2[b][:, :, hh, 0:128], Vf2[b][:, :, hh * 128:(hh + 1) * 128])),
                          deps=[mvf, vsh2_free[b], pool_ones[b]])
                if hh == 1:
                    vf2_free[b] = mv
            jb.update(b=b, V=V_, KT=KT_, mts=mts, mv=mv, hh=hh)

        def stage_SS(jb, n):
            h, s_ = jb["h"], jb["s"]
            KT_ = jb["KT"]
            Sm = [bk(2 + m_)[:, 0:256].rearrange("p (t q) -> p t q", q=16) for m_ in range(2)]
            q0 = CS0 + 16 * s_
            mm = None
            for kt in range(16):
                for m_ in range(2):
                    lhsT = KT_[64 * m_:64 * m_ + 64, kt * 128:(kt + 1) * 128]
                    rhs = QT[64 * m_:64 * m_ + 64, h, q0:q0 + 16]
                    outp = Sm[m_][:, kt, :]
                    mm = S.op("pe", (lambda e, outp=outp, lhsT=lhsT, rhs=rhs, kt=kt: e.matmul(outp, lhsT, rhs, start=True, stop=(kt != 15))),
                              deps=jb["mts"] + [bank_free[2], bank_free[3]] if (kt == 0 and m_ == 0) else [],
                              inc=(kt == 15 and m_ == 1))
            ktsh_free[0] = mm
            for m_ in range(2):
                outp = Sm[m_][:, 15, :]
                for part in range(2):
                    rhs = bias_t["P"][part][:, h, 0:16]
                    mm = S.op("pe", (lambda e, outp=outp, rhs=rhs, part=part: e.matmul(
                        outp, identb[:, :], rhs, start=False, stop=(part == 1))),
                        deps=[], inc=(m_ == 1 and part == 1))
            jb.update(Sm=Sm, mm=mm, q0=q0)

        def stage_SX(jb, n):
            pb = n % 2
            P_ = Pts[pb]
            Sm, mm = jb["Sm"], jb["mm"]
            mexa = []
            for m_ in range(2):
                mexa.append(S.op("act", (lambda e, P_=P_, src=Sm[m_], m_=m_: e.activation(
                    P_[:, m_, :, :], src, AF.Exp, scale=0.125)), deps=[mm, pts_free[pb]]))
                bank_free[2 + m_] = mexa[-1]
            jb.update(P=P_, mexa=mexa, pb=pb)

        def stage_SAV(jb, n):
            h, s_ = jb["h"], jb["s"]
            b = jb["b"]
            P_, V_ = jb["P"], jb["V"]
            pe_par = n % 2
            ob = (4, 5) if pe_par == 0 else (6, 7)
            O1 = bk(ob[0])[0:16, 0:129]
            O2 = bk(ob[1])[0:16, 0:129]
            Os = (O1, O2)
            mm2 = None
            for kt in range(16):
                for m_ in range(2):
                    lhsT = P_[:, m_, kt, :]
                    rhs = V_[:, kt, :]
                    dps = []
                    if kt == 0:
                        dps = [bank_free[ob[m_]]]
                        if m_ == 0:
                            dps += jb["mexa"] + [jb["mv"]] + head_ptn[h]
                    outp = Os[m_]
                    mm2 = S.op("pe", (lambda e, outp=outp, lhsT=lhsT, rhs=rhs, kt=kt: e.matmul(
                        outp, lhsT, rhs, start=(kt == 0), stop=False)), deps=dps, inc=False)
            for m_ in range(2):
                lhsT = Ptn_all[0:64, h, m_, 16 * s_:16 * s_ + 16]
                rhs = vnew[0:64, h, :]
                outp = Os[m_]
                mm2 = S.op("pe", (lambda e, outp=outp, lhsT=lhsT, rhs=rhs: e.matmul(outp, lhsT, rhs, start=False, stop=True)),
                           deps=[], inc=(m_ == 1))
            pts_free[jb["pb"]] = mm2
            if jb["hh"] == 1:
                vsh2_free[b] = mm2
            ptn_used[0] = mm2
            zsrc = zass[0:16, s_, h * 128:(h + 1) * 128]
            epilogue(16, O1, O2, zsrc, (jb["q0"], jb["q0"] + 16), h, pe_par, ob, mm2, n)

        kf_free = [None, None]
        vf_free = [None, None]
        sample_start = []
        if jobs:
            for h_ in range(NH):
                stage_N(h_)
            sample_start.append(S.last["pe"])
            for v_ in Vsh2:
                pool_ones.append(S.op("pool", (lambda e, v_=v_: e.memset(v_[:, :, :, 128:129], 1.0)), deps=sample_start))
            pair_dma(0)
            stage_T(jobs[0], 0)
            stage_SS(jobs[0], 0)
        for n, jb in enumerate(jobs):
            stage_SX(jb, n)
            flush_B(n)
            if n + 1 < len(jobs):
                nj = jobs[n + 1]
                if nj["h"] != jb["h"]:
                    pass
                stage_T(nj, n + 1)
            flush_pending(n)
            stage_SAV(jb, n)
            if n + 1 < len(jobs):
                nj = jobs[n + 1]
                stage_SS(nj, n + 1)
        flush_B(0, force=True)
        flush_pending(0, force=True)
        d_done = [S.last["act"], S.last["dve"], S.last["pool"], S.last["pe"]]

    if _ph("C2"):
        for e_ in ("pe", "act", "dve", "pool"):
            S.wait(e_, d_done)
        wsl_free = [None, None]
        pre = {}
        pre[8] = load_slab(8, extra_deps=d_done)
        groups = [(0, 512), (512, 512), (1024, 80)]
        u_free = [None, None]
        g_free = [None, None]
        h_free = [None, None]
        z_free = [None, None]
        hz_n = [0]
        y_free = [None]
        wpa_free = [None, None]
        pre_e = {}

        def load_e(g):
            b_, m1 = load_slab(16 + g, extra_deps=d_done)
            pb2 = g % 2
            m2 = S.dma("pool", f"wpa{pb2}", wpa[pb2][:, :, :], wpa_l[g], deps=[wpa_free[pb2]] + d_done)
            m3 = S.dma("pool", f"wpc{pb2}", wpc[pb2][:, :, :], wpc_l[g], deps=[wpa_free[pb2]] + d_done)
            return b_, [m1, m2, m3]

        for c in range(8):
            sl = 8 + c
            if c + 1 < 8:
                pre[sl + 1] = load_slab(sl + 1, extra_deps=d_done)
            elif _ph("E"):
                pre_e[0] = load_e(0)
            wb, m_w = pre[sl]
            w = wsl[wb]
            ub = c % 2
            U = ubuf[ub]
            G = gbuf[ub]
            Us = U[:, 1040:1112].rearrange("p (s t) -> p s t", t=18)
            m_st = S.op("pool", (lambda e, Us=Us, c=c: e.tensor_copy(Us[:, :, 0:2], scT[:, c, :, :])),
                        deps=[u_free[ub], ld["scT"]])
            u_ms = [m_st]
            g_ms = []
            last_mm = None
            for (c0, n) in groups:
                bks = [next_bank() for _ in range(4)]
                outs = [bk(b_)[:, 0:n] for b_ in bks]
                mms = []
                for part in range(4):
                    for dc in range(16):
                        lhsT = w[:, dc, part * 128:(part + 1) * 128]
                        rhs = xnT[:, dc, c0:c0 + n]
                        outp = outs[part]
                        mm = S.op("pe", (lambda e, outp=outp, lhsT=lhsT, rhs=rhs, dc=dc: e.matmul(
                            outp, lhsT, rhs, start=(dc == 0), stop=(dc == 15))),
                            deps=[m_w, bank_free[bks[part]]] if dc == 0 else [], inc=(dc == 15))
                    mms.append(mm)
                last_mm = mm
                hb_ = hz_n[0] % 2
                hz_n[0] += 1
                Hs = c2h[hb_][:, 0:n]
                Zs = c2z[hb_][:, 0:n]
                mh = S.op("act", (lambda e, Hs=Hs, src=outs[2]: e.copy(Hs, src)), deps=[mms[2], h_free[hb_]])
                bank_free[bks[2]] = mh
                mz = S.op("act", (lambda e, Zs=Zs, src=outs[3]: e.activation(Zs, src, AF.Silu)), deps=[mms[3], z_free[hb_]])
                bank_free[bks[3]] = mz
                if c0 < 1024:
                    b0 = c0 // 128
                    Uv = U[:, 0:1040].rearrange("p (b t) -> p b t", t=130)[:, b0:b0 + 4, 2:130]
                    mu = S.op("dve", (lambda e, Uv=Uv, src=outs[1], Hs=Hs: e.tensor_tensor(
                        Uv, src.rearrange("p (b t) -> p b t", t=128), Hs.rearrange("p (b t) -> p b t", t=128), ALU.mult)),
                        deps=[mms[1], mh, u_free[ub]])
                    u_ms.append(mu)
                    lastu = mu
                else:
                    Uh = U[:, 0:1040].rearrange("p (b t) -> p b t", t=130)[:, :, 0:2]
                    mu1 = S.op("dve", (lambda e, Uh=Uh, src=outs[1], Hs=Hs: e.tensor_tensor(
                        Uh, src[:, 0:16].rearrange("p (b t) -> p b t", t=2), Hs[:, 0:16].rearrange("p (b t) -> p b t", t=2), ALU.mult)),
                        deps=[mms[1], mh, u_free[ub]])
                    Usm = Us[:, :, 2:18]
                    mu2 = S.op("dve", (lambda e, Usm=Usm, src=outs[1], Hs=Hs: e.tensor_tensor(
                        Usm, src[:, 16:80].rearrange("p (s t) -> p s t", t=16), Hs[:, 16:80].rearrange("p (s t) -> p s t", t=16), ALU.mult)),
                        deps=[mms[1], mh, u_free[ub]])
                    u_ms += [mu1, mu2]
                    lastu = mu2
                bank_free[bks[1]] = lastu
                h_free[hb_] = lastu
                mg = S.op("dve", (lambda e, G=G, src=outs[0], Zs=Zs, c0=c0, n=n: e.tensor_tensor(
                    G[:, c0:c0 + n], src, Zs, ALU.mult)), deps=[mms[0], mz, g_free[ub]])
                bank_free[bks[0]] = mg
                z_free[hb_] = mg
                g_ms.append(mg)
            wsl_free[wb] = last_mm
            Ub = U[:, 0:1040].rearrange("p (b t) -> p b t", t=130)
            Yp = ybuf[:, 0:1024].rearrange("p (b t) -> p b t", t=128)
            Ys = ybuf[:, CS0:CS0 + 64].rearrange("p (s t) -> p s t", t=16)
            w0 = convw[:, c, 0:1]
            w1 = convw[:, c, 1:2]
            w2 = convw[:, c, 2:3]
            ydeps = u_ms + [y_free[0], ld["convw"]]
            y1 = S.op("dve", (lambda e, Ub=Ub, Yp=Yp, w2=w2: e.tensor_scalar(Yp, Ub[:, :, 2:130], w2, None, ALU.mult)), deps=ydeps)
            y1s = S.op("dve", (lambda e, Us=Us, Ys=Ys, w2=w2: e.tensor_scalar(Ys, Us[:, :, 2:18], w2, None, ALU.mult)), deps=ydeps)
            y2 = S.op("dve", (lambda e, Ub=Ub, Yp=Yp, w1=w1: e.scalar_tensor_tensor(Yp, Ub[:, :, 1:129], w1, Yp, ALU.mult, ALU.add)), deps=[y1])
            y2s = S.op("dve", (lambda e, Us=Us, Ys=Ys, w1=w1: e.scalar_tensor_tensor(Ys, Us[:, :, 1:17], w1, Ys, ALU.mult, ALU.add)), deps=[y1s])
            y3 = S.op("dve", (lambda e, Ub=Ub, Yp=Yp, w0=w0: e.scalar_tensor_tensor(Yp, Ub[:, :, 0:128], w0, Yp, ALU.mult, ALU.add)), deps=[y2])
            y3s = S.op("dve", (lambda e, Us=Us, Ys=Ys, w0=w0: e.scalar_tensor_tensor(Ys, Us[:, :, 0:16], w0, Ys, ALU.mult, ALU.add)), deps=[y2s])
            n1 = S.op("pool", (lambda e, Ub=Ub, c=c: e.tensor_copy(ncv[:, c, 0:1, :], Ub[:, 7:8, 128:130])), deps=u_ms)
            n2 = S.op("pool", (lambda e, Us=Us, c=c: e.tensor_copy(ncv[:, c, 1:5, :], Us[:, :, 16:18])), deps=u_ms)
            u_free[ub] = [y3, y3s, n1, n2]
            r1 = S.op("dve", (lambda e, G=G, c=c: e.tensor_tensor(rT[:, c, 0:1024], G[:, 0:1024], ybuf[:, 0:1024], ALU.mult)),
                      deps=g_ms + [y3])
            r2 = S.op("dve", (lambda e, G=G, c=c: e.tensor_tensor(rT[:, c, CS0:CS0 + 64], G[:, CS0:CS0 + 64], ybuf[:, CS0:CS0 + 64], ALU.mult)),
                      deps=g_ms + [y3s])
            g_free[ub] = [r1, r2]
            y_free[0] = [r1, r2]
        for r_ in range(5):
            for j_ in range(2):
                m_cv = S.dma("sp", "cvo", cvo[r_, j_].rearrange("(c p) -> p c", p=128), ncv[:, :, r_, j_],
                             deps=[S.last["pool"]], allow_slow_non_contiguous=True)
        out_ms.append(m_cv)
        c2_done = [S.last["act"], S.last["dve"], S.last["pool"], S.last["pe"]]

    pre_f = {}

    def load_wo(cg):
        b_ = wsl_n[0] % 2
        wsl_n[0] += 1
        m_ = S.dma("pool", f"wsl{b_}", wsl[b_][:, :, :], wo_l[:, :, cg * 512:(cg + 1) * 512], deps=[wsl_free[b_]])
        return b_, m_

    if _ph("E"):
        for e_ in ("pe", "act", "dve", "pool"):
            S.wait(e_, c2_done)
        e_groups = [(0, 512), (512, 512), (CS0, 64)]
        sg_free = [None] * 4
        t_free = [None] * 4
        en = [0]
        pre = dict(pre_e)
        for g in range(8):
            if g + 1 < 8:
                pre[g + 1] = load_e(g + 1)
            elif _ph("F"):
                pre_f[0] = load_wo(0)
            wb, m_w = pre[g]
            w = wsl[wb]
            A_ = wpa[g % 2]
            C_ = wpc[g % 2]
            last_mm = None
            for jj in range(2):
                j = 2 * g + jj
                for (c0, n) in e_groups:
                    par_ = en[0] % 2
                    en[0] += 1
                    bks = [4 * par_ + k for k in range(4)]
                    outs = [bk(b_)[:, 0:n] for b_ in bks]
                    mms = []
                    for ec in range(8):
                        mm = S.op("pe", (lambda e, outp=outs[0], lhsT=A_[:, ec, jj * 128:(jj + 1) * 128], rhs=QT[:, ec, c0:c0 + n], ec=ec:
                                         e.matmul(outp, lhsT, rhs, start=(ec == 0), stop=(ec == 7))),
                                  deps=m_w + [bank_free[bks[0]]] if ec == 0 else [], inc=(ec == 7))
                    mms.append(mm)
                    for ec in range(8):
                        mm = S.op("pe", (lambda e, outp=outs[1], lhsT=C_[:, ec, jj * 128:(jj + 1) * 128], rhs=rT[:, ec, c0:c0 + n], ec=ec:
                                         e.matmul(outp, lhsT, rhs, start=(ec == 0), stop=(ec == 7))),
                                  deps=m_w + [bank_free[bks[1]]] if ec == 0 else [], inc=(ec == 7))
                    mms.append(mm)
                    for gi in range(2):
                        for dc in range(16):
                            mm = S.op("pe", (lambda e, outp=outs[2 + gi], lhsT=w[:, dc, gi * 256 + jj * 128:gi * 256 + (jj + 1) * 128],
                                             rhs=xnT[:, dc, c0:c0 + n], dc=dc:
                                             e.matmul(outp, lhsT, rhs, start=(dc == 0), stop=(dc == 15))),
                                      deps=m_w + [bank_free[bks[2 + gi]]] if dc == 0 else [], inc=(dc == 15))
                        mms.append(mm)
                    last_mm = mm
                    sb0 = 2 * par_
                    sga = e_sg[sb0][:, 0:n]
                    sgc = e_sg[sb0 + 1][:, 0:n]
                    t1 = e_t[sb0][:, 0:n]
                    t2 = e_t[sb0 + 1][:, 0:n]
                    ms1 = S.op("act", (lambda e, sga=sga, src=outs[2]: e.activation(sga, src, AF.Sigmoid)), deps=[mms[2], sg_free[sb0]])
                    bank_free[bks[2]] = ms1
                    ms2 = S.op("act", (lambda e, sgc=sgc, src=outs[3]: e.activation(sgc, src, AF.Sigmoid)), deps=[mms[3], sg_free[sb0 + 1]])
                    bank_free[bks[3]] = ms2
                    mt1 = S.op("dve", (lambda e, t1=t1, src=outs[0], sga=sga: e.tensor_tensor(t1, src, sga, ALU.mult)),
                               deps=[mms[0], ms1, t_free[sb0]])
                    bank_free[bks[0]] = mt1
                    sg_free[sb0] = mt1
                    mt2 = S.op("dve", (lambda e, t2=t2, src=outs[1], sgc=sgc: e.tensor_tensor(t2, src, sgc, ALU.mult)),
                               deps=[mms[1], ms2, t_free[sb0 + 1]])
                    bank_free[bks[1]] = mt2
                    sg_free[sb0 + 1] = mt2
                    mad = S.op("pool", (lambda e, t1=t1, t2=t2, j=j, c0=c0, n=n: e.tensor_tensor(mT[:, j, c0:c0 + n], t1, t2, ALU.add)),
                               deps=[mt1, mt2])
                    t_free[sb0] = mad
                    t_free[sb0 + 1] = mad
            wsl_free[wb] = last_mm
            wpa_free[g % 2] = last_mm
        e_done = [S.last["act"], S.last["dve"], S.last["pool"], S.last["pe"]]

    if _ph("F"):
        for e_ in ("pe", "act", "dve", "pool", "sp"):
            S.wait(e_, e_done)
        m_gp = S.dma("sp", "gpost", gpost[:, :], g_post.partition_broadcast(128), deps=e_done)
        m_gps = S.op("dve", lambda e: e.tensor_scalar(gpost[:, :], gpost[:, :], float(math.sqrt(D)), None, ALU.mult), deps=[m_gp])
        xh_free = [None, None]
        yh_free = [None, None]
        hn = [0]
        xld = {}

        def load_xh(k):
            t_, hf_ = k // 2, k % 2
            rows_ = 128 if t_ < 8 else 64
            hb_ = k % 2
            xsrc = xo[t_ * 128:(t_ + 1) * 128, hf_ * 1024:(hf_ + 1) * 1024] if t_ < 8 else xhs[16:80, hf_ * 1024:(hf_ + 1) * 1024]
            xld[k] = S.dma("sp", f"xh{hb_}", xh[hb_][0:rows_, :], xsrc, deps=[xh_free[hb_]] + e_done)

        load_xh(0)
        load_xh(1)

        pre = {0: pre_f[0]} if 0 in pre_f else {0: load_wo(0)}
        last_mm = {}
        fj_prev = [None]

        def f_group(cg, t):
            wb, m_w = pre[cg]
            w = wsl[wb]
            rows = 128 if t < 8 else 64
            c0 = t * 128 if t < 8 else CS0
            bnk = next_bank()
            outp = bk(bnk)[0:rows, :]
            mm = None
            for dc in range(16):
                mm = S.op("pe", (lambda e, outp=outp, lhsT=mT[:, dc, c0:c0 + rows], rhs=w[:, dc, :], dc=dc:
                                 e.matmul(outp, lhsT, rhs, start=(dc == 0), stop=(dc == 15))),
                          deps=[m_w, bank_free[bnk]] if dc == 0 else [], inc=(dc == 15))
            last_mm[cg] = mm
            ydst = y_acc[0:rows, t, cg * 512:(cg + 1) * 512]
            mev = evac(ydst, outp, [mm])
            bank_free[bnk] = mev
            msq = S.op("act", (lambda e: e.activation(
                fjunk[0:rows, :], ydst, AF.Square, accum_out=ssF[0:rows, t, cg:cg + 1])), deps=[mev, fj_prev[0]])
            fj_prev[0] = msq
            return mev, msq

        def f_epilogue(t, mev, msq):
            rows = 128 if t < 8 else 64
            m_s1 = S.op("dve", (lambda e: e.tensor_reduce(ssF1[0:rows, t:t + 1], ssF[0:rows, t, :],
                                                          mybir.AxisListType.X, ALU.add)), deps=[msq])
            m_rsa = S.op("act", (lambda e: e.activation(rsF[0:rows, t:t + 1], ssF1[0:rows, t:t + 1],
                                                        AF.Ln, bias=epsD[0:rows, 0:1])), deps=[m_s1, m_epsD])
            m_rs = S.op("act", (lambda e: e.activation(rsF[0:rows, t:t + 1], rsF[0:rows, t:t + 1],
                                                       AF.Exp, scale=-0.5)), deps=[m_rsa])
            for hf in range(2):
                kk_ = 2 * t + hf
                hb = kk_ % 2
                m_x = xld[kk_]
                Y = yh[hb]
                m_y = S.op("dve", (lambda e, Y=Y, hf=hf: e.scalar_tensor_tensor(
                    Y[0:rows, :], y_acc[0:rows, t, hf * 1024:(hf + 1) * 1024], rsF[0:rows, t:t + 1],
                    gpost[0:rows, hf * 1024:(hf + 1) * 1024], ALU.mult, ALU.mult)),
                    deps=[m_rs, m_gps, yh_free[hb], mev])
                m_add = S.op("pool", (lambda e, Y=Y, hb=hb: e.tensor_tensor(
                    Y[0:rows, :], Y[0:rows, :], xh[hb][0:rows, :], ALU.add)), deps=[m_y, m_x])
                xh_free[hb] = m_add
                dsty = yo[t * 128:(t + 1) * 128, hf * 1024:(hf + 1) * 1024] if t < 8 else ys[:, hf * 1024:(hf + 1) * 1024]
                m_st = S.dma("sp", f"yh{hb}", dsty, Y[0:rows, :], deps=[m_add])
                yh_free[hb] = m_st
                out_ms.append(m_st)
                if kk_ + 2 < 18:
                    load_xh(kk_ + 2)

        for cg in range(2):
            pre[cg + 1] = load_wo(cg + 1)
            for t in range(9):
                f_group(cg, t)
            wsl_free[pre[cg][0]] = last_mm[cg]
        pre[3] = load_wo(3)
        LAG = 1
        for k_ in range(9 + LAG):
            if k_ < 9:
                f_group(2, k_)
            if k_ - LAG >= 0:
                mev, msq = f_group(3, k_ - LAG)
                f_epilogue(k_ - LAG, mev, msq)
        wsl_free[pre[2][0]] = last_mm[2]
        wsl_free[pre[3][0]] = last_mm[3]

    S.wait("sp", out_ms)
    S.wait("sp", S.all_last())

    sem_names = set(S.ENG)
    for e_ in S.ENG:
        for it in S.q[e_]:
            if it[0] == "wait":
                sem_names.add(it[1])
            elif it[0] == "dma":
                sem_names.add(it[3])
    sem_names = sorted(sem_names)
    sems = {n: nc.alloc_semaphore(f"s_{n}") for n in sem_names}

    def run_queue(eng_obj, items, own):
        for it in items:
            if it[0] == "wait":
                eng_obj.wait_ge(sems[it[1]], it[2])
            elif it[0] == "op":
                ins = it[1](eng_obj)
                if it[2]:
                    ins.then_inc(sems[own], 1)
            else:
                _, out_, in_, key, kw = it
                eng_obj.dma_start(out=out_, in_=in_, **kw).then_inc(sems[key], 16)

    with nc.Block() as block:
        @block.tensor
        def _(e):
            run_queue(e, S.q["pe"], "pe")

        @block.scalar
        def _(e):
            run_queue(e, S.q["act"], "act")

        @block.vector
        def _(e):
            run_queue(e, S.q["dve"], "dve")

        @block.gpsimd
        def _(e):
            run_queue(e, S.q["pool"], "pool")

        @block.sync
        def _(e):
            run_queue(e, S.q["sp"], "sp")

    return nc


_PROGRAM = {}


def _get_program():
    key = (LAST_PHASE, DEBUG_DUMPS)
    if key not in _PROGRAM:
        _PROGRAM[key] = build_program()
    return _PROGRAM[key]


def kernel(x_prompt, x_sample, cache_k, cache_v, state_conv, norm_pre, norm_post,
           w_in, lambda_q1, lambda_k1, lambda_q2, lambda_k2, head_norm, conv_w,
           w_proj_attn, w_proj_conv, w_out, rel_bias):
    f32 = np.float32
    x_prompt = np.asarray(x_prompt, f32)
    x_sample = np.asarray(x_sample, f32)
    cache_k = np.asarray(cache_k, f32)
    cache_v = np.asarray(cache_v, f32)
    state_conv = np.asarray(state_conv, f32)
    w_in = np.asarray(w_in, f32)

    perm = _w_in_perm()
    w_l = np.ascontiguousarray(
        w_in[0][:, perm].reshape(16, 128, 24, 512).transpose(2, 1, 0, 3))
    wpa_l = np.ascontiguousarray(
        np.asarray(w_proj_attn, f32)[0].reshape(8, 128, 8, 256).transpose(2, 1, 0, 3))
    wpc_l = np.ascontiguousarray(
        np.asarray(w_proj_conv, f32)[0].reshape(8, 128, 8, 256).transpose(2, 1, 0, 3))
    wo_l = np.ascontiguousarray(
        np.asarray(w_out, f32)[0].reshape(16, 128, 2048).transpose(1, 0, 2))
    lam_in = np.concatenate([np.asarray(a, f32).reshape(-1) for a in
                             (lambda_q1, lambda_k1, lambda_q2, lambda_k2)]).reshape(1, 256)
    ident = np.eye(128, dtype=f32)
    ohd = np.concatenate([_onehot_window(0)] * 2, axis=0)
    ohp = np.concatenate([_onehot_window(-128)] * 2, axis=0)
    kk = np.arange(128)
    maskd = ((kk[:, None] // 64) <= (kk[None, :] // 64)).astype(f32)
    k64 = np.arange(64)
    bmask = ((k64[:, None] // 16) == (k64[None, :] // 16)).astype(f32)

    shared = {
        "w_l": w_l, "wpa_l": wpa_l, "wpc_l": wpc_l, "wo_l": wo_l,
        "g_pre": np.asarray(norm_pre, f32).reshape(1, D),
        "g_post": np.asarray(norm_post, f32).reshape(1, D),
        "g_head": np.asarray(head_norm, f32).reshape(1, 128),
        "conv_w": np.ascontiguousarray(np.asarray(conv_w, f32)[0]),
        "lam_in": lam_in,
        "rel_b": np.ascontiguousarray(np.asarray(rel_bias, f32)),
        "c_ident": ident, "c_ohd": ohd, "c_ohp": ohp, "c_maskd": maskd, "c_bmask": bmask,
    }
    in_maps = []
    for c in range(NCORES):
        b, par = c // 2, c % 2
        xb = x_prompt[b].reshape(16, 128, D)
        own = np.ascontiguousarray(xb[par::2].reshape(NOWN, D))
        oth = np.ascontiguousarray(xb[1 - par::2].reshape(NOWN, D))
        xhs = np.zeros((80, D), f32)
        for i in range(8):
            start = (2 * i + par) * 128
            if start >= 2:
                xhs[2 * i:2 * i + 2] = x_prompt[b, start - 2:start]
        xhs[16:80] = x_sample[4 * c:4 * c + 4].reshape(64, D)
        m = dict(shared)
        m.update({
            "xo": own, "xt": oth, "xhs": xhs,
            "ck": np.ascontiguousarray(
                cache_k[0, 4 * c:4 * c + 4].reshape(4, 16, 128, 4, 256).transpose(0, 3, 2, 1, 4)),
            "cv": np.ascontiguousarray(
                cache_v[0, 4 * c:4 * c + 4].reshape(4, 16, 128, 4, 256).transpose(0, 3, 2, 1, 4)),
            "sc": np.ascontiguousarray(state_conv[0, 4 * c:4 * c + 4]),
            "c_par": np.full((128, 1), float(par), f32),
        })
        in_maps.append(m)

    nc = _get_program()
    res = run_bass_kernel_spmd(nc, in_maps, core_ids=list(range(NCORES)))
    R = res.results

    y_prompt = np.zeros((4, 2048, D), f32)
    y_sample = np.zeros((32, 16, D), f32)
    nk_p = np.zeros((1, 4, 2048, NH, 128), f32)
    nv_p = np.zeros((1, 4, 2048, NH, 128), f32)
    nc_p = np.zeros((1, 4, 2, 1024), f32)
    nk_s = np.zeros((1, 32, 16, NH, 128), f32)
    nv_s = np.zeros((1, 32, 16, NH, 128), f32)
    nc_s = np.zeros((1, 32, 2, 1024), f32)
    for c in range(NCORES):
        b, par = c // 2, c % 2
        r = R[c]
        y_prompt[b].reshape(16, 128, D)[par::2] = np.asarray(r["yo"]).reshape(8, 128, D)
        y_sample[4 * c:4 * c + 4] = np.asarray(r["ys"]).reshape(4, 16, D)
        nk_p[0, b].reshape(16, 128, NH, 128)[par::2] = np.asarray(r["ko"]).reshape(8, 128, NH, 128)
        nv_p[0, b].reshape(16, 128, NH, 128)[par::2] = np.asarray(r["vo"]).reshape(8, 128, NH, 128)
        nk_s[0, 4 * c:4 * c + 4] = np.asarray(r["kso"]).reshape(4, 16, NH, 128)
        nv_s[0, 4 * c:4 * c + 4] = np.asarray(r["vso"]).reshape(4, 16, NH, 128)
        cvo = np.asarray(r["cvo"])
        if par == 1:
            nc_p[0, b] = cvo[0]
        nc_s[0, 4 * c:4 * c + 4] = cvo[1:5]
    return (y_prompt, y_sample, nk_p, nv_p, nc_p, nk_s, nv_s, nc_s)
```
